# Optimizing a Trainium2 kernel written in Bass

```python
import jax, jax.numpy as jnp
from jax import lax
import numpy as np

D_MODEL = 1024
BATCH = 8
SEQ = 2048
DEPTH = 1
DEC_BATCH = 128
DEC_SEQ = 1
PAST_LEN = 2048
PAGE_SIZE = 128

D_MIX = D_MODEL
ATTN_WIDTH = D_MIX // 2
POOL_WIDTH = D_MIX - ATTN_WIDTH
HEAD_DIM = 64
N_HEADS = ATTN_WIDTH // HEAD_DIM
KV_HEADS = 2
IDX_HEADS = 8
IDX_DIM = 64
TOPK_MAX = 256
Q_BLOCK = 128
POOL_WINDOWS = (2, 4, 8, 16)
POOL_GROUPS = len(POOL_WINDOWS)
POOL_GC = POOL_WIDTH // POOL_GROUPS
POOL_HIST = max(POOL_WINDOWS) - 1
EPS = 1e-6
N_PAGES = PAST_LEN // PAGE_SIZE
N_PHYS_PAGES = (DEC_BATCH * N_PAGES * 5) // 4
COLS = (ATTN_WIDTH, KV_HEADS * HEAD_DIM, KV_HEADS * HEAD_DIM, IDX_HEADS * IDX_DIM, IDX_DIM, IDX_HEADS,
        ATTN_WIDTH, POOL_WIDTH, POOL_WIDTH)
N_IN = sum(COLS)
SPLITS = tuple(int(s) for s in np.cumsum(COLS)[:-1])

kernel_name = 'hymba_dsa_pool_adaln_step'


def rms_norm(x, w):
    xf = x.astype(jnp.float32)
    y = xf * lax.rsqrt(jnp.mean(xf * xf, axis=-1, keepdims=True) + EPS)
    return y.astype(x.dtype) * w


def _sparse_attend_block(q, qi, wi, k, v, ki, q_pos, topk):
    B, Q, H, Dh = q.shape
    L = k.shape[1]
    k_pos = jnp.arange(L)
    s = jnp.einsum('bqhd,bld->bqhl', qi, ki)
    score = jnp.einsum('bqh,bqhl->bql', wi, jax.nn.relu(s)).astype(jnp.float32)
    causal = k_pos[None, :] <= q_pos[:, None]
    score = jnp.where(causal[None], score, -jnp.inf)
    _, sel = lax.top_k(score, topk)
    valid = sel <= q_pos[None, :, None]
    gather = jax.vmap(lambda t, i: t[i])
    kg = gather(k, sel)
    vg = gather(v, sel)
    qg = q.reshape(B, Q, KV_HEADS, H // KV_HEADS, Dh)
    logits = jnp.einsum('bqgrd,bqkgd->bqgrk', qg, kg).astype(jnp.float32) * (Dh ** -0.5)
    logits = jnp.where(valid[:, :, None, None, :], logits, -jnp.inf)
    p = jax.nn.softmax(logits, axis=-1).astype(v.dtype)
    o = jnp.einsum('bqgrk,bqkgd->bqgrd', p, vg)
    return o.reshape(B, Q, H * Dh)


def sparse_attention(q, qi, wi, k, v, ki, q_pos, topk):
    B, T = q.shape[:2]
    if T <= Q_BLOCK or T % Q_BLOCK != 0:
        return _sparse_attend_block(q, qi, wi, k, v, ki, q_pos, topk)
    nb = T // Q_BLOCK

    def to_blocks(a):
        return jnp.moveaxis(a.reshape((B, nb, Q_BLOCK) + a.shape[2:]), 1, 0)

    def body(args):
        qb, qib, wib, pb = args
        return _sparse_attend_block(qb, qib, wib, k, v, ki, pb, topk)

    o = lax.map(body, (to_blocks(q), to_blocks(qi), to_blocks(wi), q_pos.reshape(nb, Q_BLOCK)))
    return jnp.moveaxis(o, 0, 1).reshape(B, T, -1)


def multi_scale_pool(u_ext, pos, w_pool, pool_scale):
    B, E, C = u_ext.shape
    T = E - POOL_HIST
    uf = u_ext.astype(jnp.float32)
    cs = jnp.concatenate([jnp.zeros((B, 1, C), jnp.float32), jnp.cumsum(uf, axis=1)], axis=1)
    u_new = uf[:, POOL_HIST:]
    outs = []
    for g, w in enumerate(POOL_WINDOWS):
        lo, hi = g * POOL_GC, (g + 1) * POOL_GC
        end = cs[:, POOL_HIST + 1:POOL_HIST + 1 + T, lo:hi]
        start = cs[:, POOL_HIST + 1 - w:POOL_HIST + 1 - w + T, lo:hi]
        cnt = jnp.minimum(pos + 1, w).astype(jnp.float32)[None, :, None]
        outs.append((end - start) / cnt - u_new[..., lo:hi])
    d = jnp.stack(outs, axis=2).astype(u_ext.dtype)
    y = jnp.einsum('btgc,gcd->btgd', d, w_pool).reshape(B, T, C)
    return y * pool_scale


def mixer_layer(x, c, u_hist, k_past, v_past, ki_past, pos, topk,
                norm_w, w_ada, b_ada, w_in, q_norm_w, k_norm_w, w_pool, pool_scale, w_out):
    B, T, _ = x.shape
    ada = jax.nn.silu(c) @ w_ada + b_ada
    shift, scale, gate = jnp.split(ada, 3, axis=-1)
    h = rms_norm(x, norm_w) * (1 + scale[:, None]) + shift[:, None]
    proj = h @ w_in
    q, k, v, qi, ki, wi, ga, u, gp = jnp.split(proj, SPLITS, axis=-1)
    q = rms_norm(q.reshape(B, T, N_HEADS, HEAD_DIM), q_norm_w)
    k = rms_norm(k.reshape(B, T, KV_HEADS, HEAD_DIM), k_norm_w)
    v = v.reshape(B, T, KV_HEADS, HEAD_DIM)
    qi = qi.reshape(B, T, IDX_HEADS, IDX_DIM)
    wi = wi * ((IDX_HEADS * IDX_DIM) ** -0.5)
    k_all = jnp.concatenate([k_past, k], axis=1)
    v_all = jnp.concatenate([v_past, v], axis=1)
    ki_all = jnp.concatenate([ki_past, ki], axis=1)
    a_out = sparse_attention(q, qi, wi, k_all, v_all, ki_all, pos, topk) * jax.nn.silu(ga)
    u_ext = jnp.concatenate([u_hist, u], axis=1)
    p_out = multi_scale_pool(u_ext, pos, w_pool, pool_scale) * jax.nn.silu(gp)
    y = jnp.concatenate([a_out, p_out], axis=-1) @ w_out
    return x + gate[:, None] * y, k, v, ki, u_ext[:, -POOL_HIST:]


def setup_inputs(seed: int = 0) -> dict:
    key = jax.random.key(seed)
    ks = jax.random.split(key, 20)
    f32 = jnp.float32
    nrm = lambda k, s, sc: jax.random.normal(k, s, f32) * sc
    perm = jax.random.permutation(ks[0], N_PHYS_PAGES)
    page_table = perm[:DEC_BATCH * N_PAGES].reshape(DEC_BATCH, N_PAGES).astype(jnp.int32)
    return {
        'x_prompt': nrm(ks[1], (BATCH, SEQ, D_MODEL), 1.0),
        'x_sample': nrm(ks[2], (DEC_BATCH, DEC_SEQ, D_MODEL), 1.0),
        'cache_k': nrm(ks[3], (DEPTH, N_PHYS_PAGES, PAGE_SIZE, KV_HEADS, HEAD_DIM), 1.0),
        'cache_v': nrm(ks[4], (DEPTH, N_PHYS_PAGES, PAGE_SIZE, KV_HEADS, HEAD_DIM), 1.0),
        'cache_kidx': nrm(ks[5], (DEPTH, N_PHYS_PAGES, PAGE_SIZE, IDX_DIM), 1.0),
        'state_pool': nrm(ks[6], (DEPTH, DEC_BATCH, POOL_HIST, POOL_WIDTH), 1.0),
        'page_table': page_table,
        'c_prompt': nrm(ks[7], (BATCH, D_MODEL), 1.0),
        'c_sample': nrm(ks[8], (DEC_BATCH, D_MODEL), 1.0),
        'norm_w': 1.0 + nrm(ks[9], (DEPTH, D_MODEL), 0.1),
        'w_ada': nrm(ks[10], (DEPTH, D_MODEL, 3 * D_MODEL), D_MODEL ** -0.5 * 0.5),
        'b_ada': nrm(ks[11], (DEPTH, 3 * D_MODEL), 0.02),
        'w_in': nrm(ks[12], (DEPTH, D_MODEL, N_IN), D_MODEL ** -0.5),
        'q_norm_w': 1.0 + nrm(ks[13], (DEPTH, HEAD_DIM), 0.1),
        'k_norm_w': 1.0 + nrm(ks[14], (DEPTH, HEAD_DIM), 0.1),
        'w_pool': nrm(ks[15], (DEPTH, POOL_GROUPS, POOL_GC, POOL_GC), POOL_GC ** -0.5),
        'pool_scale': 1.0 + nrm(ks[16], (DEPTH, POOL_WIDTH), 0.1),
        'w_out': nrm(ks[17], (DEPTH, D_MIX, D_MODEL), D_MIX ** -0.5),
    }


def reference(x_prompt, x_sample, cache_k, cache_v, cache_kidx, state_pool, page_table, c_prompt, c_sample,
              norm_w, w_ada, b_ada, w_in, q_norm_w, k_norm_w, w_pool, pool_scale, w_out):
    Bp, S, _ = x_prompt.shape
    Bs, T, _ = x_sample.shape
    n_pages = page_table.shape[1]
    past = n_pages * cache_k.shape[2]
    topk_p = min(TOPK_MAX, S // 4)
    topk_s = min(TOPK_MAX, (past + T) // 4)
    pos_p = jnp.arange(S)
    pos_s = past + jnp.arange(T)
    dt = x_prompt.dtype
    empty_kv = jnp.zeros((Bp, 0, KV_HEADS, HEAD_DIM), dt)
    empty_ki = jnp.zeros((Bp, 0, IDX_DIM), dt)
    zero_hist = jnp.zeros((Bp, POOL_HIST, POOL_WIDTH), dt)
    xp, xs = x_prompt, x_sample
    kp, vp, kip, up = [], [], [], []
    kss, vss, kiss, uss = [], [], [], []
    for l in range(DEPTH):
        params = (norm_w[l], w_ada[l], b_ada[l], w_in[l], q_norm_w[l], k_norm_w[l], w_pool[l], pool_scale[l], w_out[l])
        xp, k1, v1, ki1, u1 = mixer_layer(xp, c_prompt, zero_hist, empty_kv, empty_kv, empty_ki, pos_p, topk_p, *params)
        k_past = cache_k[l][page_table].reshape(Bs, past, KV_HEADS, HEAD_DIM)
        v_past = cache_v[l][page_table].reshape(Bs, past, KV_HEADS, HEAD_DIM)
        ki_past = cache_kidx[l][page_table].reshape(Bs, past, IDX_DIM)
        xs, k2, v2, ki2, u2 = mixer_layer(xs, c_sample, state_pool[l], k_past, v_past, ki_past, pos_s, topk_s, *params)
        kp.append(k1); vp.append(v1); kip.append(ki1); up.append(u1)
        kss.append(k2); vss.append(v2); kiss.append(ki2); uss.append(u2)
    return (xp, xs, jnp.stack(kp), jnp.stack(vp), jnp.stack(kip), jnp.stack(up),
            jnp.stack(kss), jnp.stack(vss), jnp.stack(kiss), jnp.stack(uss))
```

```python
from contextlib import ExitStack
import numpy as np
import concourse.bass as bass
import concourse.mybir as mybir
from concourse.bass_utils import run_bass_kernel_spmd

F32 = mybir.dt.float32
F32R = mybir.dt.float32r
BF16 = mybir.dt.bfloat16
I32 = mybir.dt.int32
ALU = mybir.AluOpType
AF = mybir.ActivationFunctionType
AX = mybir.AxisListType

ENGS = ['pe', 'act', 'dve', 'pool', 'sp']
DBG_MAXOPS = 10 ** 9
DBG_DUMP = False

D = 1024
T = 2048
NT = 16
NS = 16
NPG = 16
NIN = 2888
EPS = 1e-6
NEG = -30000.0
BIGNEG = -1.0e30
NIT = 10
TOPK = 256


class Buf:
    __slots__ = ('name', 'w', 'r', 'excl')

    def __init__(self, name='', excl=False):
        self.name = name
        self.w = None
        self.r = []
        self.excl = excl


class Sched:
    def __init__(self, nc, stack, n_dma_sems=48):
        self.nc = nc
        self.q = {e: [] for e in ENGS}
        self.tick = {e: 0 for e in ENGS}
        self.seen = {e: {} for e in ENGS}
        self.esem = {e: stack.enter_context(nc.semaphore('es_' + e)) for e in ENGS}
        self.dsem = [stack.enter_context(nc.semaphore('ds%d' % i)) for i in range(n_dma_sems)]
        self.dcnt = [0] * n_dma_sems
        self.dnext = 0
        self.dnext_q = {}
        self.barrier_events = []

    def _collect(self, eng, reads, writes):
        evs = []
        for b in reads:
            if b.w is not None:
                evs.append(b.w)
            if b.excl:
                evs.extend(x for x in b.r if not (x[0] == 'eng' and x[1] == eng))
        for b in writes:
            if b.w is not None:
                evs.append(b.w)
            evs.extend(b.r)
        seen = self.seen[eng]
        m = {}
        for ev in evs:
            if ev[0] == 'eng':
                _, e, t = ev
                if e == eng and eng == 'pe':
                    continue
                key = ('eng', e)
                sem = self.esem[e]
            else:
                _, s, t = ev
                key = ('dma', s)
                sem = self.dsem[s]
            if seen.get(key, 0) >= t:
                continue
            if key not in m or m[key][1] < t:
                m[key] = (sem, t)
        for key, (sem, t) in m.items():
            seen[key] = t
        return list(m.values())

    def _record(self, ev, reads, writes):
        for b in reads:
            if ev[0] == 'eng':
                b.r = [x for x in b.r if not (x[0] == 'eng' and x[1] == ev[1])]
            b.r.append(ev)
        for b in writes:
            b.w = ev
            b.r = []

    def op(self, eng, fn, reads=(), writes=()):
        self.nops = getattr(self, 'nops', 0) + 1
        if self.nops > DBG_MAXOPS:
            return
        waits = self._collect(eng, reads, writes)
        self.tick[eng] += 1
        t = self.tick[eng]
        sem = self.esem[eng]

        def run(e, waits=waits, fn=fn, sem=sem):
            for s, v in waits:
                e.wait_ge(s, v)
            fn(e).then_inc(sem, 1)
        self.q[eng].append(run)
        self._record(('eng', eng, t), reads, writes)

    def dma(self, queue, out, in_, reads=(), writes=(), fn=None, **kw):
        self.nops = getattr(self, 'nops', 0) + 1
        if self.nops > DBG_MAXOPS:
            return
        waits = self._collect(queue, reads, writes)
        lo_, hi_ = (0, len(self.dsem) // 2) if queue == 'sp' else (len(self.dsem) // 2, len(self.dsem))
        nx = self.dnext_q.get(queue, lo_)
        s = nx
        self.dnext_q[queue] = lo_ + (nx + 1 - lo_) % (hi_ - lo_)
        prev = self.dcnt[s] * 16
        self.dcnt[s] += 1
        v = self.dcnt[s] * 16
        sem = self.dsem[s]
        if prev > 0 and self.seen[queue].get(('dma', s), 0) < prev:
            self.seen[queue][('dma', s)] = prev
            waits = [w for w in waits if w[0] is not sem] + [(sem, prev)]

        def run(e, waits=waits, sem=sem, out=out, in_=in_, kw=kw, fn=fn):
            for s_, v_ in waits:
                e.wait_ge(s_, v_)
            if fn is not None:
                fn(e).then_inc(sem, 16)
            else:
                e.dma_start(out=out, in_=in_, **kw).then_inc(sem, 16)
        self.q[queue].append(run)
        self._record(('dma', s, v), reads, writes)

    def barrier(self):
        ev = [('eng', e, self.tick[e]) for e in ENGS if self.tick[e] > 0]
        ev += [('dma', s, c * 16) for s, c in enumerate(self.dcnt) if c > 0 and s < len(self.dsem) // 2]
        self.barrier_events = ev

    def build(self):
        nc = self.nc
        fin = [(self.dsem[s], c * 16) for s, c in enumerate(self.dcnt) if c > 0]
        fin += [(self.esem[e], self.tick[e]) for e in ENGS if self.tick[e] > 0]

        def last(e, fin=fin):
            for s_, v_ in fin:
                e.wait_ge(s_, v_)
        self.q['sp'].append(last)
        with nc.Block() as block:
            @block.sync
            def _(e):
                for f in self.q['sp']:
                    f(e)

            @block.tensor
            def _(e):
                for f in self.q['pe']:
                    f(e)

            @block.scalar
            def _(e):
                for f in self.q['act']:
                    f(e)

            @block.vector
            def _(e):
                for f in self.q['dve']:
                    f(e)

            @block.gpsimd
            def _(e):
                for f in self.q['pool']:
                    f(e)


class TL:
    def __init__(self, t, nb=1, name='', excl=False, init_r=()):
        self.t = t
        self.bs = [Buf(name + str(i), excl) for i in range(nb)]
        for b in self.bs:
            b.r = list(init_r)

    @property
    def b(self):
        return self.bs[0]


def _pool_mats():
    wins = (2, 4, 8, 16)
    m0 = np.zeros((4, 128, 128), np.float32)
    mc = np.zeros((4, 128, 128), np.float32)
    mp = np.zeros((4, 128, 128), np.float32)
    tp = np.arange(128)[:, None]
    t = np.arange(128)[None, :]
    for g, w in enumerate(wins):
        band = ((t - tp) >= 0) & ((t - tp) < w)
        cnt0 = np.minimum(t + 1, w).astype(np.float32)
        m0[g] = band / cnt0 - (t == tp)
        mc[g] = band / np.float32(w) - (t == tp)
        bandp = ((t + 128 - tp) >= 0) & ((t + 128 - tp) < w)
        mp[g] = bandp / np.float32(w)
    return m0, mc, mp


def _consts():
    m0, mc, mp = _pool_mats()
    c = {}
    c['pmat'] = np.ascontiguousarray(np.stack([m0, mc, mp], 0).transpose(2, 0, 1, 3)).reshape(128, 3 * 4 * 128)
    ident = np.eye(128, dtype=np.float32)
    c['ident'] = ident
    c['ident4'] = np.concatenate([ident] * 4, axis=1)
    tri = np.where(np.arange(128)[None, :] <= np.arange(128)[:, None], 0.0, BIGNEG).astype(np.float32)
    c['causal'] = tri
    offd = np.where(np.eye(16) > 0, 0.0, BIGNEG).astype(np.float32)
    c['offdiag'] = offd
    c['bsel'] = np.repeat(np.eye(16, dtype=np.float32), 8, axis=1)
    c['pow2'] = (2.0 ** -(np.arange(NIT, dtype=np.float32) + 1.0))[None, :].repeat(128, 0).astype(np.float32)
    return c


def _win_perm():
    segs = {}
    off = 0
    for name, w in (('q', 512), ('k', 128), ('v', 128), ('qi', 512), ('ki', 64), ('wi', 8),
                    ('ga', 512), ('u', 512), ('gp', 512)):
        segs[name] = np.arange(off, off + w)
        off += w
    hord = [g * 4 + hp for hp in range(4) for g in range(2)]
    for n in ('q', 'qi'):
        segs[n] = np.concatenate([segs[n][h * 64:(h + 1) * 64] for h in hord])
    segs['wi'] = segs['wi'][hord]
    order = ['q', 'qi', 'ga', 'gp', 'u', 'k', 'v', 'ki', 'wi']
    return np.concatenate([segs[n] for n in order])


def build_program(phases='0ASB'):
    nc = bass.Bass("TRN2", target_bir_lowering=False)
    din = lambda n, s, d=F32: nc.dram_tensor(n, s, d, kind="ExternalInput").ap()
    dout = lambda n, s, d=F32: nc.dram_tensor(n, s, d, kind="ExternalOutput").ap()
    x_d = din("x", [T, D])
    xs_d = din("xs", [NS, D])
    cv_d = din("cvec", [33, D])
    if 'S' in phases:
        ck_d = din("cache_k", [2560 * 128, 128])
        cvv_d = din("cache_v", [2560 * 128, 128])
        cki_d = din("cache_ki", [2560 * 128, 64])
    sp_d = din("state_pool", [NS, 15 * 512])
    pt_d = din("page_table", [1, NS * NPG], I32)
    nw_d = din("norm_w", [1, D])
    wada_d = din("w_ada", [D, 3 * D])
    bada_d = din("b_ada", [1, 3 * D])
    win_d = din("w_in", [D, NIN])
    qw_d = din("q_norm_w", [1, 64])
    kw_d = din("k_norm_w", [1, 64])
    wpool_d = din("w_pool", [4, 128, 128])
    psc_d = din("pool_scale", [1, 512])
    wout_d = din("w_out", [D, D])
    pmat_d = din("pmat", [128, 1536])
    ident_d = din("ident", [128, 128])
    ident4_d = din("ident4", [128, 512])
    causal_d = din("causal", [128, 128])
    offd_d = din("offdiag", [16, 16])
    bsel_d = din("bsel", [16, 128])
    pow2_d = din("pow2", [128, NIT])

    yp_d = dout("y_p", [T, D])
    ys_d = dout("y_s", [NS, D])
    kp_d = dout("k_p", [T, 128])
    vp_d = dout("v_p", [T, 128])
    kip_d = dout("ki_p", [T, 64])
    pp_d = dout("pool_p", [15, 512])
    ks_d = dout("k_s", [NS, 128])
    vs_d = dout("v_s", [NS, 128])
    kis_d = dout("ki_s", [NS, 64])
    ps_d = dout("pool_s", [NS, 15 * 512])

    dbg = {}
    if DBG_DUMP:
        dbg['ada'] = dout("dbg_ada", [33, 3 * D])
        dbg['A_p'] = dout("dbg_A_p", [128, D])
        dbg['B_p'] = dout("dbg_B_p", [128, D])
        dbg['h1'] = dout("dbg_h1", [128, D])
        dbg['hb'] = dout("dbg_hb", [128, D])
        dbg['hT'] = dout("dbg_hT", [128, 1024])
        dbg['win'] = dout("dbg_win", [128, NIN])
        dbg['stt'] = dout("dbg_stt", [128, 40])
        dbg['nw'] = dout("dbg_nw", [128, D])
        dbg['sc'] = dout("dbg_sc", [128, 512])
    with ExitStack() as st:
        S = Sched(nc, st)
        _cnt = [0]

        def sb(shape, dt, nb=1, name=None, stack=st):
            _cnt[0] += 1
            name = name or ('t%d' % _cnt[0])
            return TL(stack.enter_context(nc.sbuf_tensor(name, list(shape), dt)), nb, name, init_r=S.barrier_events)

        def pst(shape, dt, nb=1, name=None, stack=st):
            _cnt[0] += 1
            name = name or ('p%d' % _cnt[0])
            return TL(stack.enter_context(nc.psum_tensor(name, list(shape), dt)), nb, name, excl=True)

        PB = [pst([128, 512], F32, name='bank%d' % i) for i in range(8)]

        def bf_view(bank):
            return bank.t[:].bitcast(BF16)

        ident_f = sb([128, 128], F32)
        ident_b = sb([128, 128], BF16)
        ident4_b = sb([128, 512], BF16)
        causal = sb([128, 128], F32)
        offd = sb([16, 16], F32)
        bsel = sb([16, 128], F32)
        pow2 = sb([128, NIT], F32)
        ones_f = sb([128, 128], F32)
        ones_b = sb([128, 128], BF16)
        qw_bc = sb([128, 512], F32)
        kw_bc = sb([128, 128], F32)
        ada_sb = sb([33, 3 * D], F32)
        s0A = st.enter_context(ExitStack())
        A_p = sb([128, D], F32, stack=s0A)
        B_p = sb([128, D], F32, stack=s0A)
        A_s = sb([NS, D], F32, stack=s0A)
        eps_t = sb([128, 1], F32)

        S.dma('sp', ident_f.t[:], ident_d, writes=[ident_f.b])
        S.dma('pool', ident_b.t[:], ident_d, writes=[ident_b.b])
        S.dma('pool', ident4_b.t[:], ident4_d, writes=[ident4_b.b])
        S.dma('sp', causal.t[:], causal_d, writes=[causal.b])
        S.dma('sp', offd.t[:], offd_d, writes=[offd.b])
        S.dma('sp', bsel.t[:], bsel_d, writes=[bsel.b])
        S.dma('sp', pow2.t[:], pow2_d, writes=[pow2.b])
        for h in range(8):
            S.dma('sp', qw_bc.t[:, h * 64:(h + 1) * 64], qw_d.partition_broadcast(128), writes=[qw_bc.b])
        for g in range(2):
            S.dma('sp', kw_bc.t[:, g * 64:(g + 1) * 64], kw_d.partition_broadcast(128), writes=[kw_bc.b])
        S.op('dve', lambda e: e.memset(ones_f.t[:], 1.0), writes=[ones_f.b])
        S.op('dve', lambda e: e.memset(ones_b.t[:], 1.0), writes=[ones_b.b])
        S.op('dve', lambda e: e.memset(eps_t.t[:], EPS), writes=[eps_t.b])
        S.op('dve', lambda e: e.tensor_scalar(out=qw_bc.t[:], in0=qw_bc.t[:], scalar1=0.125, scalar2=None, op0=ALU.mult),
             reads=[qw_bc.b], writes=[qw_bc.b])

        qT_all = sb([128, NT, 512], BF16, NT)
        qiT_all = sb([128, NT, 512], F32R, NT)
        kT_all = sb([128, T], BF16, NT)
        kiT_all = sb([128, T], F32R, NT)
        V_all = sb([128, NT, 2, 128], BF16, NT)
        zs_d = nc.dram_tensor("zs_scratch", [T, 1024], BF16, kind="Internal").ap()
        zs_bufs = [Buf('zs%d' % i) for i in range(NT)]
        wis_all = sb([128, NT, 8], F32, NT)
        sgn_all = sb([128, NT, 8], F32, NT)
        qn_s = sb([NS, 512], F32)
        qis_s = sb([NS, 512], F32)
        kn_s = sb([NS, 128], F32)
        v_s = sb([NS, 128], F32)
        ki_s = sb([NS, 64], F32)
        wis_s = sb([NS, 8], F32)
        sga_s = sb([NS, 512], BF16)
        zp_s = sb([NS, 512], BF16)

        S.op('pool', lambda e: e.memset(V_all.t[:, :, :, 64:128], 0.0), writes=V_all.bs)
        S.op('pool', lambda e: e.memset(V_all.t[:, :, :, 64:65], 1.0), writes=V_all.bs)

        scr_bufs = {}
        if 'S' in phases:
            scrK = nc.dram_tensor("scr_k", [128, NS * NPG, 128], F32, kind="Internal").ap()
            scrV = nc.dram_tensor("scr_v", [128, NS * NPG, 128], F32, kind="Internal").ap()
            scrKi = nc.dram_tensor("scr_ki", [128, NS * NPG, 64], F32, kind="Internal").ap()
            GW = 512
            ptc = sb([128, 2], I32)
            ptf = sb([128, 2], F32)
            idxg = sb([128, 2, 32], I32)
            idxi = sb([128, 2, 16], I32)
            qoff = sb([128, 32], F32)
            qoff_i = sb([128, 32], I32)
            stg = [sb([128, GW], F32) for _ in range(2)]
            ptf8 = sb([128, 4], F32)
        sW = ExitStack()
        win_bf = sb([128, 8, NIN], BF16, 8, stack=sW)
        pmat = sb([128, 3, 4, 128], BF16, stack=sW)
        sWa = ExitStack()
        wa = [sb([128, 3 * D], BF16, stack=sWa) for _ in range(2)]
        for k in range(2):
            for hf in range(2):
                S.dma('pool', wa[k].t[:, hf * 1536:(hf + 1) * 1536], wada_d[k * 128:(k + 1) * 128, hf * 1536:(hf + 1) * 1536], writes=[wa[k].b])
        with ExitStack() as s0:
            c_sb = sb([33, D], F32, stack=s0)
            normw_bc = sb([128, D], F32, stack=s0)
            S.dma('sp', normw_bc.t[:], nw_d.partition_broadcast(128), writes=[normw_bc.b])
            sc = sb([33, D], F32, stack=s0)
            scT = sb([128, 8, 33], BF16, stack=s0)
            bada_bc = sb([33, 3 * D], F32, stack=s0)
            S.dma('sp', c_sb.t[:], cv_d, writes=[c_sb.b])
            S.dma('sp', bada_bc.t[:], bada_d.partition_broadcast(33), writes=[bada_bc.b])
            S.op('act', lambda e: e.activation(out=sc.t[:], in_=c_sb.t[:], func=AF.Silu), reads=[c_sb.b], writes=[sc.b])
            for k in range(8):
                S.op('pe', lambda e, k=k: e.transpose(out=PB[7].t[:, k * 33:(k + 1) * 33], in_=sc.t[0:33, k * 128:(k + 1) * 128],
                                                      identity=ident_f.t[0:33, 0:33]),
                     reads=[sc.b, ident_f.b], writes=[PB[7].b])
            S.op('dve', lambda e: e.tensor_copy(out=scT.t[:].rearrange("p k m -> p (k m)"), in_=PB[7].t[:, 0:8 * 33]),
                 reads=[PB[7].b], writes=[scT.b])
            for k in range(8):
                w = wa[k % 2]
                if k >= 2:
                    for hf in range(2):
                        S.dma('pool', w.t[:, hf * 1536:(hf + 1) * 1536], wada_d[k * 128:(k + 1) * 128, hf * 1536:(hf + 1) * 1536], writes=[w.b])
                for n in range(6):
                    S.op('pe', lambda e, k=k, n=n, w=w: e.matmul(PB[n].t[0:33, :], lhsT=scT.t[:, k, :], rhs=w.t[:, n * 512:(n + 1) * 512],
                                                                 start=(k == 0), stop=(k == 7)),
                         reads=[scT.b, w.b], writes=[PB[n].b])
            for k in range(8):
                for hf in range(2):
                    c0, c1 = hf * 1444, (hf + 1) * 1444
                    S.dma('pool', win_bf.t[:, k, c0:c1], win_d[k * 128:(k + 1) * 128, c0:c1], writes=[win_bf.bs[k]])
            S.dma('pool', pmat.t[:].rearrange("p a g t -> p (a g t)"), pmat_d, writes=[pmat.b])
            for n in range(6):
                S.op('dve', lambda e, n=n: e.tensor_tensor(out=ada_sb.t[:, n * 512:(n + 1) * 512], in0=PB[n].t[0:33, :],
                                                           in1=bada_bc.t[:, n * 512:(n + 1) * 512], op=ALU.add),
                     reads=[PB[n].b, bada_bc.b], writes=[ada_sb.b])
            for n in range(4):
                S.op('pe', lambda e, n=n: e.matmul(PB[n].t[:, :], lhsT=ones_f.t[32:33, 0:128], rhs=ada_sb.t[32:33, n * 512:(n + 1) * 512],
                                                   start=True, stop=True),
                     reads=[ones_f.b, ada_sb.b], writes=[PB[n].b])
            if DBG_DUMP:
                S.dma('sp', dbg['nw'], normw_bc.t[:], reads=[normw_bc.b])
                S.op('act', lambda e: e.copy(out=sc.t[0:33, 0:512], in_=PB[2].t[0:33, :]), reads=[PB[2].b], writes=[sc.b])
                S.dma('sp', dbg['sc'][0:33, :], sc.t[0:33, 0:512], reads=[sc.b])
            for hf in range(2):
                cs = slice(hf * 512, (hf + 1) * 512)
                S.op('act', lambda e, hf=hf, cs=cs: e.copy(out=B_p.t[:, cs], in_=PB[hf].t[:, :]), reads=[PB[hf].b], writes=[B_p.b])
                S.op('dve', lambda e, hf=hf, cs=cs: e.scalar_tensor_tensor(out=A_p.t[:, cs], in0=PB[2 + hf].t[:, :], scalar=1.0, in1=normw_bc.t[:, cs],
                                                                           op0=ALU.add, op1=ALU.mult),
                     reads=[PB[2 + hf].b, normw_bc.b], writes=[A_p.b])
            S.op('dve', lambda e: e.scalar_tensor_tensor(out=A_s.t[:, :], in0=ada_sb.t[0:NS, D:2 * D], scalar=1.0, in1=normw_bc.t[0:NS, :],
                                                         op0=ALU.add, op1=ALU.mult),
                 reads=[ada_sb.b, normw_bc.b], writes=[A_s.b])

        if DBG_DUMP:
            S.dma('sp', dbg['ada'], ada_sb.t[:], reads=[ada_sb.b])
            S.dma('sp', dbg['A_p'], A_p.t[:], reads=[A_p.b])
            S.dma('sp', dbg['B_p'], B_p.t[:], reads=[B_p.b])
        sWa.close()
        S.barrier()
        if 'S' in phases:
            for hf in range(2):
                S.dma('sp', ptc.t[:, hf:hf + 1], pt_d[:, hf * 128:(hf + 1) * 128].rearrange("o p -> p o"), writes=[ptc.b])
            S.op('pool', lambda e: e.iota(out=qoff_i.t[:], pattern=[[1, 32]], base=0, channel_multiplier=0), writes=[qoff_i.b])
            S.op('dve', lambda e: e.tensor_copy(out=qoff.t[:], in_=qoff_i.t[:]), reads=[qoff_i.b], writes=[qoff.b])
            S.op('dve', lambda e: e.tensor_copy(out=ptf.t[:], in_=ptc.t[:]), reads=[ptc.b], writes=[ptf.b])
            S.op('dve', lambda e: e.tensor_scalar(out=ptf8.t[:, 0:2], in0=ptf.t[:, :], scalar1=32.0, scalar2=None, op0=ALU.mult), reads=[ptf.b], writes=[ptf8.b])
            S.op('dve', lambda e: e.tensor_scalar(out=ptf8.t[:, 2:4], in0=ptf.t[:, :], scalar1=16.0, scalar2=None, op0=ALU.mult), reads=[ptf.b], writes=[ptf8.b])
            for hf in range(2):
                S.op('dve', lambda e, hf=hf: e.tensor_scalar(out=idxg.t[:, hf, :], in0=qoff.t[:, 0:32], scalar1=ptf8.t[:, hf:hf + 1], scalar2=None, op0=ALU.add),
                     reads=[qoff.b, ptf8.b], writes=[idxg.b])
                S.op('dve', lambda e, hf=hf: e.tensor_scalar(out=idxi.t[:, hf, :], in0=qoff.t[:, 0:16], scalar1=ptf8.t[:, 2 + hf:3 + hf], scalar2=None, op0=ALU.add),
                     reads=[qoff.b, ptf8.b], writes=[idxi.b])
            gi = 0
            for name, cache, scr, npc, idxt in (('ki', cki_d, scrKi, 16, None), ('k', ck_d, scrK, 32, None), ('v', cvv_d, scrV, 32, None)):
                scr_bufs[name] = Buf('scr_' + name)
                cview = cache.rearrange("(n q) d -> n (q d)", q=GW // cache.shape[1])
                for hf in range(2):
                    for pc in range(npc):
                        sg = stg[gi % 2]
                        gi += 1
                        ix = (idxi if npc == 16 else idxg)
                        col = ix.t[:, hf, pc:pc + 1]

                        def fn(e, sg=sg, cview=cview, col=col):
                            return e.indirect_dma_start(out=sg.t[:, :], out_offset=None, in_=cview,
                                                        in_offset=bass.IndirectOffsetOnAxis(ap=col, axis=0))
                        S.dma('pool', None, None, reads=[ix.b], writes=[sg.b], fn=fn)
                        dd = cache.shape[1]
                        tt = GW // dd
                        S.dma('pool', scr[pc * tt:(pc + 1) * tt, hf * 128:(hf + 1) * 128, :].rearrange("t p d -> p t d"),
                              sg.t[:, :].rearrange("p (t d) -> p t d", d=dd), reads=[sg.b], writes=[scr_bufs[name]])

        with ExitStack() as sA:
            wpool_bf = sb([128, 4, 128], BF16, stack=sA)
            h1 = sb([128, D], F32, stack=sA)
            S.dma('sp', h1.t[:, 0:512].rearrange("p (g d) -> p g d", g=4), wpool_d.rearrange("g c d -> c g d"), writes=[h1.b])
            S.dma('sp', h1.t[:, 512:1024], psc_d.partition_broadcast(128), writes=[h1.b])
            S.op('dve', lambda e: e.tensor_tensor(out=wpool_bf.t[:].rearrange("p g d -> p (g d)"), in0=h1.t[:, 0:512],
                                                  in1=h1.t[:, 512:1024], op=ALU.mult),
                 reads=[h1.b], writes=[wpool_bf.b])

            xt = [sb([128, D], F32, stack=sA) for _ in range(2)]
            zst = [sb([128, 1024], BF16, stack=sA) for _ in range(1)]
            st_ = [sb([128, 40], F32, stack=sA) for _ in range(2)]
            hb = sb([128, D], BF16, stack=sA)
            hT = [sb([128, 8, 128], BF16, stack=sA) for _ in range(2)]
            tq = sb([128, 512], F32, stack=sA)
            qraw = sb([128, 512], F32, stack=sA)
            kvraw = sb([128, 328], F32, stack=sA)
            qn = sb([128, 512], BF16, stack=sA)
            qis = sb([128, 512], F32, stack=sA)
            sgp = sb([128, 512], BF16, stack=sA)
            u_f = sb([128, 512], F32, stack=sA)
            u_bf = [sb([128, 512], BF16, stack=sA) for _ in range(2)]
            dT = sb([128, 512], BF16, stack=sA)
            kv = [sb([128, 400], F32, stack=sA) for _ in range(1)]
            kb = sb([128, 128], BF16, stack=sA)
            ki2 = sb([128, 128], F32, stack=sA)
            state4 = sb([128, 15, 128], F32, stack=sA)
            un4 = sb([128, 128], F32, stack=sA)
            d4 = sb([128, 128], F32, stack=sA)
            d4b = sb([128, 128], BF16, stack=sA)
            S.op('dve', lambda e: e.memset(state4.t[:], 0.0), writes=[state4.b])
            S.op('dve', lambda e: e.memset(un4.t[:], 0.0), writes=[un4.b])
            S.op('dve', lambda e: e.memset(d4.t[:], 0.0), writes=[d4.b])
            S.op('dve', lambda e: e.memset(d4b.t[:], 0.0), writes=[d4b.b])
            for g in range(4):
                S.dma('sp', state4.t[g * 32:g * 32 + NS, :, :], sp_d.rearrange("b (r c) -> b r c", r=15)[:, :, g * 128:(g + 1) * 128],
                      writes=[state4.b])
            S.dma('sp', ps_d[:, 0:14 * 512], sp_d[:, 512:15 * 512])

            class Defer:
                def __init__(self):
                    self.early, self.late = [], []

                def op(self, *a, early=False, **k):
                    (self.early if early else self.late).append(('op', a, k))

                def dma(self, *a, early=False, **k):
                    (self.early if early else self.late).append(('dma', a, k))

                def flush(self, which):
                    lst = self.early if which == 'early' else self.late
                    for kind, a, k in lst:
                        (S.op if kind == 'op' else S.dma)(*a, **k)
                    lst.clear()
            defer = {}

            def phaseA_tile(i, stage):
                samp = (i == NT)
                R = NS if samp else 128
                x = xt[i % 2]
                stt = st_[i % 2]
                kvt = kv[0]
                Asrc, Bsrc = (A_s.t[0:R, :], ada_sb.t[0:R, 0:D]) if samp else (A_p.t[:, :], B_p.t[:, :])
                Ab, Bb = (A_s.b, ada_sb.b) if samp else (A_p.b, B_p.b)
                if stage == 'load':
                    S.dma('sp', x.t[0:R, :], xs_d if samp else x_d[i * 128:(i + 1) * 128, :], writes=[x.b])
                    return
                hTt = hT[i % 2]
                if stage == 'pre':
                    phaseA_pre(i, samp, R, x, stt, Asrc, Bsrc, Ab, Bb, hTt)
                    return
                widths = [512, 512, 512, 512, 512, 328]
                if stage == 'proj':
                    for n in (5, 0, 1, 2, 3, 4):
                        c0 = n * 512
                        for k in range(8):
                            S.op('pe', lambda e, n=n, k=k, c0=c0: e.matmul(PB[n].t[0:R, 0:widths[n]], lhsT=hTt.t[:, k, 0:R],
                                                                           rhs=win_bf.t[:, k, c0:c0 + widths[n]], start=(k == 0), stop=(k == 7)),
                                 reads=[hTt.b, win_bf.bs[k]], writes=[PB[n].b])
                    return
                if stage == 'postA':
                    defer[i] = Defer()
                    phaseA_post(i, samp, R, x, stt, kvt, defer[i])
                    defer[i].flush('early')
                    return
                defer.pop(i).flush('late')

            def phaseA_pre(i, samp, R, x, stt, Asrc, Bsrc, Ab, Bb, hTt):
                S.op('act', lambda e: e.activation(out=h1.t[0:R, :], in_=x.t[0:R, :], func=AF.Square, accum_out=stt.t[0:R, 0:1]),
                     reads=[x.b], writes=[h1.b, stt.b])
                S.op('act', lambda e: e.activation(out=stt.t[0:R, 1:2], in_=stt.t[0:R, 0:1], func=AF.Ln, bias=eps_t.t[0:R, 0:1], scale=1.0 / D),
                     reads=[stt.b, eps_t.b], writes=[stt.b])
                S.op('act', lambda e: e.activation(out=stt.t[0:R, 2:3], in_=stt.t[0:R, 1:2], func=AF.Exp, scale=-0.5),
                     reads=[stt.b], writes=[stt.b])
                S.op('dve', lambda e: e.scalar_tensor_tensor(out=h1.t[0:R, :], in0=x.t[0:R, :], scalar=stt.t[0:R, 2:3], in1=Asrc,
                                                             op0=ALU.mult, op1=ALU.mult),
                     reads=[x.b, stt.b, Ab], writes=[h1.b])
                S.op('dve', lambda e: e.tensor_tensor(out=hb.t[0:R, :], in0=h1.t[0:R, :], in1=Bsrc, op=ALU.add),
                     reads=[h1.b, Bb], writes=[hb.b])
                tb = PB[6]
                for k in range(8):
                    S.op('pe', lambda e, k=k: e.transpose(out=bf_view(tb)[:, k * 128:k * 128 + R], in_=hb.t[0:R, k * 128:(k + 1) * 128],
                                                          identity=ident_b.t[0:R, 0:R]),
                         reads=[hb.b, ident_b.b], writes=[tb.b])
                S.op('act', lambda e: e.copy(out=hTt.t[:, :, 0:R], in_=bf_view(tb).rearrange("p (k r) -> p k r", k=8)[:, :, 0:R]),
                     reads=[tb.b], writes=[hTt.b])
                if DBG_DUMP and i == 0:
                    S.dma('sp', dbg['h1'], h1.t[:], reads=[h1.b])
                    S.dma('pool', dbg['hb'], hb.t[:], reads=[hb.b])
                    S.dma('pool', dbg['hT'], hTt.t[:].rearrange("p k r -> p (k r)"), reads=[hTt.b])
                    S.dma('pool', dbg['win'], win_bf.t[:, 0, :], reads=[win_bf.bs[0]])
                    S.dma('sp', dbg['stt'], stt.t[:], reads=[stt.b])

            def phaseA_post(i, samp, R, x, stt, kvt, E):
                P5r = PB[5]
                E.op('act', lambda e: e.copy(out=kvraw.t[0:R, :], in_=P5r.t[0:R, 0:328]), reads=[P5r.b], writes=[kvraw.b], early=True)
                P5 = kvraw
                E.op('act', lambda e: e.activation(out=tq.t[0:R, 0:128], in_=P5.t[0:R, 0:128], func=AF.Square), reads=[P5.b], writes=[tq.b])
                E.op('dve', lambda e: e.tensor_reduce(out=stt.t[0:R, 4:6], in_=tq.t[0:R, 0:128].rearrange("p (g d) -> p g d", g=2), axis=AX.X, op=ALU.add),
                     reads=[tq.b], writes=[stt.b])
                E.op('act', lambda e: e.activation(out=stt.t[0:R, 6:8], in_=stt.t[0:R, 4:6], func=AF.Ln, bias=eps_t.t[0:R, 0:1], scale=1.0 / 64),
                     reads=[stt.b, eps_t.b], writes=[stt.b])
                E.op('act', lambda e: e.activation(out=stt.t[0:R, 8:10], in_=stt.t[0:R, 6:8], func=AF.Exp, scale=-0.5),
                     reads=[stt.b], writes=[stt.b])
                E.op('dve', lambda e: e.tensor_tensor(out=kvt.t[0:R, 256:384].rearrange("p (g d) -> p g d", g=2),
                                                      in0=P5.t[0:R, 0:128].rearrange("p (g d) -> p g d", g=2),
                                                      in1=stt.t[0:R, 8:10].unsqueeze(2).to_broadcast([R, 2, 64]), op=ALU.mult),
                     reads=[P5.b, stt.b], writes=[kvt.b])
                kdst = kn_s.t[0:R, :] if samp else kvt.t[0:R, 0:128]
                kdb = kn_s.b if samp else kvt.b
                E.op('dve', lambda e: e.tensor_tensor(out=kdst, in0=kvt.t[0:R, 256:384], in1=kw_bc.t[0:R, :], op=ALU.mult),
                     reads=[kvt.b, kw_bc.b], writes=[kdb])
                vdst = v_s.t[0:R, :] if samp else kvt.t[0:R, 128:256]
                vdb = v_s.b if samp else kvt.b
                E.op('act', lambda e: e.copy(out=vdst, in_=P5.t[0:R, 128:256]), reads=[P5.b], writes=[vdb])
                if samp:
                    E.op('act', lambda e: e.copy(out=ki_s.t[0:R, :], in_=P5.t[0:R, 256:320]), reads=[P5.b], writes=[ki_s.b])
                    E.op('dve', lambda e: e.tensor_scalar(out=wis_s.t[0:R, :], in0=P5.t[0:R, 320:328], scalar1=512.0 ** -0.5, scalar2=None, op0=ALU.mult),
                         reads=[P5.b], writes=[wis_s.b], early=True)
                    E.dma('sp', ks_d, kn_s.t[:, :], reads=[kn_s.b])
                    E.dma('sp', vs_d, v_s.t[:, :], reads=[v_s.b])
                    E.dma('sp', kis_d, ki_s.t[:, :], reads=[ki_s.b])
                    wsrc, wb = wis_s.t[0:R, :], wis_s.b
                else:
                    E.op('act', lambda e: e.copy(out=ki2.t[:, 0:64], in_=P5.t[:, 256:320]), reads=[P5.b], writes=[ki2.b])
                    E.op('dve', lambda e: e.tensor_copy(out=ki2.t[:, 64:128], in_=P5.t[:, 256:320]), reads=[P5.b], writes=[ki2.b])
                    E.op('dve', lambda e: e.tensor_scalar(out=wis_all.t[:, i, :], in0=P5.t[:, 320:328], scalar1=512.0 ** -0.5, scalar2=None, op0=ALU.mult),
                         reads=[P5.b], writes=[wis_all.bs[i]], early=True)
                    E.dma('sp', kp_d[i * 128:(i + 1) * 128, :], kvt.t[:, 0:128], reads=[kvt.b])
                    E.dma('sp', vp_d[i * 128:(i + 1) * 128, :], kvt.t[:, 128:256], reads=[kvt.b])
                    E.dma('sp', kip_d[i * 128:(i + 1) * 128, :], ki2.t[:, 0:64], reads=[ki2.b])
                    wsrc, wb = wis_all.t[:, i, :], wis_all.bs[i]
                    E.op('dve', lambda e: e.tensor_copy(out=kb.t[:, :], in_=kvt.t[:, 0:128]), reads=[kvt.b], writes=[kb.b])
                    E.op('act', lambda e: e.copy(out=V_all.t[:, i, :, 0:64], in_=kvt.t[:, 128:256].rearrange("p (g d) -> p g d", g=2)),
                         reads=[kvt.b], writes=[V_all.bs[i]])
                    E.op('dve', lambda e: e.tensor_scalar(out=sgn_all.t[:, i, :], in0=wis_all.t[:, i, :], scalar1=0.0, scalar2=None, op0=ALU.is_ge),
                         reads=[wis_all.bs[i]], writes=[sgn_all.bs[i]])
                    E.op('dve', lambda e: e.tensor_scalar(out=sgn_all.t[:, i, :], in0=sgn_all.t[:, i, :], scalar1=2.0, scalar2=-1.0, op0=ALU.mult, op1=ALU.add),
                         reads=[sgn_all.bs[i]], writes=[sgn_all.bs[i]])
                E.op('dve', lambda e: e.scalar_tensor_tensor(out=stt.t[0:R, 16:24], in0=wsrc, scalar=-1.0, in1=wsrc, op0=ALU.mult, op1=ALU.max),
                     reads=[wb], writes=[stt.b], early=True)
                P0r = PB[0]
                E.op('act', lambda e: e.copy(out=qraw.t[0:R, :], in_=P0r.t[0:R, :]), reads=[P0r.b], writes=[qraw.b], early=True)
                P0 = qraw
                E.op('act', lambda e: e.activation(out=tq.t[0:R, 0:512], in_=P0.t[0:R, :], func=AF.Square), reads=[P0.b], writes=[tq.b])
                E.op('dve', lambda e: e.tensor_reduce(out=stt.t[0:R, 24:32], in_=tq.t[0:R, 0:512].rearrange("p (h d) -> p h d", h=8), axis=AX.X, op=ALU.add),
                     reads=[tq.b], writes=[stt.b])
                E.op('act', lambda e: e.activation(out=stt.t[0:R, 32:40], in_=stt.t[0:R, 24:32], func=AF.Ln, bias=eps_t.t[0:R, 0:1], scale=1.0 / 64),
                     reads=[stt.b, eps_t.b], writes=[stt.b])
                E.op('act', lambda e: e.activation(out=stt.t[0:R, 24:32], in_=stt.t[0:R, 32:40], func=AF.Exp, scale=-0.5),
                     reads=[stt.b], writes=[stt.b])
                E.op('dve', lambda e: e.tensor_tensor(out=tq.t[0:R, :].rearrange("p (h d) -> p h d", h=8),
                                                      in0=P0.t[0:R, :].rearrange("p (h d) -> p h d", h=8),
                                                      in1=stt.t[0:R, 24:32].unsqueeze(2).to_broadcast([R, 8, 64]), op=ALU.mult),
                     reads=[P0.b, stt.b], writes=[tq.b])
                qdst, qdb = (qn_s.t[0:R, :], qn_s.b) if samp else (qn.t[:, :], qn.b)
                E.op('dve', lambda e: e.tensor_tensor(out=qdst, in0=tq.t[0:R, :], in1=qw_bc.t[0:R, :], op=ALU.mult),
                     reads=[tq.b, qw_bc.b], writes=[qdb])
                P1 = PB[1]
                qidst, qidb = (qis_s.t[0:R, :], qis_s.b) if samp else (qis.t[:, :], qis.b)
                E.op('dve', lambda e: e.tensor_tensor(out=qidst.rearrange("p (h d) -> p h d", h=8),
                                                      in0=P1.t[0:R, :].rearrange("p (h d) -> p h d", h=8),
                                                      in1=stt.t[0:R, 16:24].unsqueeze(2).to_broadcast([R, 8, 64]), op=ALU.mult),
                     reads=[P1.b, stt.b], writes=[qidb], early=True)
                P2, P3, P4 = PB[2], PB[3], PB[4]
                zt = zst[0]
                gdst, gdb = (sga_s.t[0:R, :], sga_s.b) if samp else (zt.t[:, 0:512], zt.b)
                E.op('act', lambda e: e.activation(out=gdst, in_=P2.t[0:R, :], func=AF.Silu), reads=[P2.b], writes=[gdb], early=True)
                E.op('act', lambda e: e.activation(out=sgp.t[0:R, :], in_=P3.t[0:R, :], func=AF.Silu), reads=[P3.b], writes=[sgp.b], early=True)
                ub = u_bf[i % 2]
                ubp = u_bf[(i + 1) % 2]
                E.op('act', lambda e: e.copy(out=u_f.t[0:R, :], in_=P4.t[0:R, :]), reads=[P4.b], writes=[u_f.b], early=True)
                if samp:
                    E.dma('sp', ps_d[:, 14 * 512:15 * 512], u_f.t[0:R, :], reads=[u_f.b])
                    for g in range(4):
                        E.dma('sp', un4.t[g * 32:g * 32 + NS, :], u_f.t[0:NS, g * 128:(g + 1) * 128], reads=[u_f.b], writes=[un4.b])
                    for g, w in enumerate((2, 4, 8, 16)):
                        ps_ = slice(g * 32, g * 32 + NS)
                        hv = state4.t[ps_, 15 - (w - 1):15, :].rearrange("p r c -> p c r")
                        E.op('dve', lambda e, hv=hv, ps_=ps_: e.tensor_reduce(out=d4.t[ps_, :], in_=hv, axis=AX.X, op=ALU.add),
                             reads=[state4.b], writes=[d4.b])
                        E.op('dve', lambda e, ps_=ps_, w=w: e.scalar_tensor_tensor(out=d4.t[ps_, :], in0=un4.t[ps_, :], scalar=(1.0 - w), in1=d4.t[ps_, :],
                                                                                  op0=ALU.mult, op1=ALU.add),
                             reads=[un4.b, d4.b], writes=[d4.b])
                        E.op('dve', lambda e, ps_=ps_, w=w: e.tensor_scalar(out=d4b.t[ps_, :], in0=d4.t[ps_, :], scalar1=1.0 / w, scalar2=None, op0=ALU.mult),
                             reads=[d4.b], writes=[d4b.b])
                    E.op('pe', lambda e: e.transpose(out=bf_view(PB[7])[:, 0:128], in_=d4b.t[:, :], identity=ident_b.t[:, :]),
                         reads=[d4b.b, ident_b.b], writes=[PB[7].b])
                    E.op('act', lambda e: e.copy(out=dT.t[:, 0:128], in_=bf_view(PB[7])[:, 0:128]), reads=[PB[7].b], writes=[dT.b])
                else:
                    E.op('dve', lambda e: e.tensor_copy(out=ub.t[:, :], in_=P4.t[:, :]), reads=[P4.b], writes=[ub.b], early=True)
                    if i == NT - 1:
                        E.dma('sp', pp_d, u_f.t[113:128, :], reads=[u_f.b])
                    for g in range(4):
                        kind = 0 if i == 0 else 1
                        E.op('pe', lambda e, g=g, kind=kind: e.matmul(PB[7].t[:, g * 128:(g + 1) * 128], lhsT=ub.t[:, g * 128:(g + 1) * 128],
                                                                      rhs=pmat.t[:, kind, g, :], start=True, stop=(i == 0)),
                             reads=[ub.b, pmat.b], writes=[PB[7].b])
                        if i > 0:
                            E.op('pe', lambda e, g=g: e.matmul(PB[7].t[:, g * 128:(g + 1) * 128], lhsT=ubp.t[:, g * 128:(g + 1) * 128],
                                                               rhs=pmat.t[:, 2, g, :], start=False, stop=True),
                                 reads=[ubp.b, pmat.b], writes=[PB[7].b])
                    E.op('act', lambda e: e.copy(out=dT.t[:, :], in_=PB[7].t[:, :]), reads=[PB[7].b], writes=[dT.b])
                for g in range(4):
                    dc = g * 32 if samp else g * 128
                    E.op('pe', lambda e, g=g, dc=dc: e.matmul(PB[7].t[0:R, g * 128:(g + 1) * 128], lhsT=dT.t[:, dc:dc + R],
                                                       rhs=wpool_bf.t[:, g, :], start=True, stop=True),
                         reads=[dT.b, wpool_bf.b], writes=[PB[7].b])
                zdst, zdb = (zp_s.t[0:R, :], zp_s.b) if samp else (zt.t[:, 512:1024], zt.b)
                E.op('dve', lambda e: e.tensor_tensor(out=zdst, in0=PB[7].t[0:R, :], in1=sgp.t[0:R, :], op=ALU.mult),
                     reads=[PB[7].b, sgp.b], writes=[zdb])
                if samp:
                    return
                E.dma('sp', zs_d[i * 128:(i + 1) * 128, :], zt.t[:, :], reads=[zt.b], writes=[zs_bufs[i]])
                tb = PB[6]
                for hh in range(4):
                    E.op('pe', lambda e, hh=hh: e.transpose(out=bf_view(tb)[:, hh * 128:(hh + 1) * 128],
                                                            in_=qn.t[:, hh * 128:(hh + 1) * 128],
                                                            identity=ident_b.t[:, :]),
                         reads=[qn.b, ident_b.b], writes=[tb.b])
                E.op('pe', lambda e: e.transpose(out=bf_view(tb)[:, 512:640], in_=kb.t[:, :], identity=ident_b.t[:, :]),
                     reads=[kb.b, ident_b.b], writes=[tb.b])
                E.op('act', lambda e: e.copy(out=qT_all.t[:, i, :], in_=bf_view(tb)[:, 0:512]), reads=[tb.b], writes=[qT_all.bs[i]])
                E.op('act', lambda e: e.copy(out=kT_all.t[:, i * 128:(i + 1) * 128], in_=bf_view(tb)[:, 512:640]), reads=[tb.b], writes=[kT_all.bs[i]])
                tb2 = PB[7]
                for hh in range(4):
                    E.op('pe', lambda e, hh=hh: e.transpose(out=tb2.t[:, hh * 128:(hh + 1) * 128],
                                                            in_=qis.t[:, hh * 128:(hh + 1) * 128],
                                                            identity=ident_f.t[:, :]),
                         reads=[qis.b, ident_f.b], writes=[tb2.b])
                E.op('dve', lambda e: e.tensor_copy(out=qiT_all.t[:, i, :], in_=tb2.t[:, :]), reads=[tb2.b], writes=[qiT_all.bs[i]])
                E.op('pe', lambda e: e.transpose(out=tb2.t[:, 0:128], in_=ki2.t[:, :], identity=ident_f.t[:, :]),
                     reads=[ki2.b, ident_f.b], writes=[tb2.b])
                E.op('act', lambda e: e.copy(out=kiT_all.t[:, i * 128:(i + 1) * 128], in_=tb2.t[:, 0:128]), reads=[tb2.b], writes=[kiT_all.bs[i]])

            if 'A' in phases:
                tl = list(DBG_TILES)
                phaseA_tile(tl[0], 'load')
                if len(tl) > 1:
                    phaseA_tile(tl[1], 'load')
                phaseA_tile(tl[0], 'pre')
                phaseA_tile(tl[0], 'proj')
                for n, i in enumerate(tl):
                    if n + 1 < len(tl):
                        phaseA_tile(tl[n + 1], 'pre')
                    if n + 2 < len(tl):
                        phaseA_tile(tl[n + 2], 'load')
                    phaseA_tile(i, 'postA')
                    if n + 1 < len(tl):
                        phaseA_tile(tl[n + 1], 'proj')
                    phaseA_tile(i, 'postB')

        sW.close()

        S.barrier()
        with ExitStack() as sB:
            wout_bf = sb([128, 8, D], BF16, 8, stack=sB)
            wstg = [sb([128, D], F32, stack=sB) for _ in range(2)]
            G_p = sb([128, D], F32, stack=sB)
            for hf in range(2):
                S.op('pe', lambda e, hf=hf: e.matmul(PB[6 + hf].t[:, :], lhsT=ones_f.t[32:33, 0:128],
                                                     rhs=ada_sb.t[32:33, 2 * D + hf * 512:2 * D + (hf + 1) * 512], start=True, stop=True),
                     reads=[ones_f.b, ada_sb.b], writes=[PB[6 + hf].b])
                S.op('act', lambda e, hf=hf: e.copy(out=G_p.t[:, hf * 512:(hf + 1) * 512], in_=PB[6 + hf].t[:, :]),
                     reads=[PB[6 + hf].b], writes=[G_p.b])
            for k in range(8):
                S.dma('sp', wstg[k % 2].t[:, :], wout_d[k * 128:(k + 1) * 128, :], writes=[wstg[k % 2].b])
                S.op('dve', lambda e, k=k: e.tensor_tensor(out=wout_bf.t[:, k, :], in0=wstg[k % 2].t[:, :], in1=G_p.t[:, :], op=ALU.mult),
                     reads=[wstg[k % 2].b, G_p.b], writes=[wout_bf.bs[k]])
            acc = [sb([128, T], F32, stack=sB) for _ in range(2)]
            Rr = [sb([128, 512], F32, stack=sB) for _ in range(2)]
            mb = [sb([128, T], BF16, stack=sB) for _ in range(2)]
            junkb = sb([128, T], BF16, stack=sB)
            PT = [sb([128, 512], BF16, stack=sB) for _ in range(6)]
            OT_sb = sb([65, 1024], F32, stack=sB)
            zt_ = sb([128, 512], BF16, stack=sB)
            zT = sb([128, 8, 128], BF16, stack=sB)
            xq = [sb([128, D], F32, stack=sB) for _ in range(2)]
            yo = [sb([128, D], F32, stack=sB) for _ in range(2)]
            zsl = [sb([128, 1024], BF16, stack=sB) for _ in range(2)]
            bs_ = [sb([128, 16 + 3 * NIT], F32, stack=sB) for _ in range(2)]
            junkb2 = None
            tmpd = sb([128, 128], F32, stack=sB)
            rden = sb([128, 8], F32, stack=sB)
            cnt_s = [0, 0]

            def phaseB_X(i, act_chain=False):
                L = 128 * (i + 1)
                nch = (L + 511) // 512
                ac = acc[i % 2]
                m_ = mb[i % 2]
                bs = bs_[i % 2]
                for j, ch in [(2 * jp + jj, ch) for jp in range(4) for ch in range(nch) for jj in range(2)]:
                    pr = slice((j % 2) * 64, (j % 2) * 64 + 64)
                    qc = slice((j // 2) * 128, (j // 2 + 1) * 128)
                    if True:
                        w = min(512, L - ch * 512)
                        cs = slice(ch * 512, ch * 512 + w)
                        bank = PB[cnt_s[0] % 2]
                        Rt = Rr[cnt_s[0] % 2]
                        cnt_s[0] += 1
                        kib = [kiT_all.bs[t] for t in range(ch * 4, min(ch * 4 + 4, i + 1))]
                        S.op('pe', lambda e, pr=pr, qc=qc, cs=cs, w=w, bank=bank: e.matmul(bank.t[:, 0:w], lhsT=qiT_all.t[pr, i, qc], rhs=kiT_all.t[pr, cs],
                                                                                          start=True, stop=True),
                             reads=[qiT_all.bs[i]] + kib, writes=[bank.b])
                        S.op('act', lambda e, w=w, bank=bank, Rt=Rt: e.activation(out=Rt.t[:, 0:w], in_=bank.t[:, 0:w], func=AF.Relu),
                             reads=[bank.b], writes=[Rt.b])
                        if j == 0:
                            S.op('dve', lambda e, w=w, cs=cs, Rt=Rt: e.tensor_scalar(out=ac.t[:, cs], in0=Rt.t[:, 0:w], scalar1=sgn_all.t[:, i, 0:1], scalar2=None,
                                                                                    op0=ALU.mult),
                                 reads=[Rt.b, sgn_all.bs[i]], writes=[ac.b])
                        else:
                            S.op('dve', lambda e, w=w, cs=cs, Rt=Rt, j=j: e.scalar_tensor_tensor(out=ac.t[:, cs], in0=Rt.t[:, 0:w], scalar=sgn_all.t[:, i, j:j + 1],
                                                                                                in1=ac.t[:, cs], op0=ALU.mult, op1=ALU.add),
                                 reads=[Rt.b, sgn_all.bs[i], ac.b], writes=[ac.b])
                dg = slice(i * 128, (i + 1) * 128)
                S.op('dve', lambda e: e.tensor_reduce(out=bs.t[:, 0:1], in_=ac.t[:, 0:L], axis=AX.X, op=ALU.max, apply_absolute_value=True), reads=[ac.b], writes=[bs.b])
                S.op('dve', lambda e: e.tensor_tensor(out=ac.t[:, dg], in0=ac.t[:, dg], in1=causal.t[:, :], op=ALU.add),
                     reads=[ac.b, causal.b], writes=[ac.b])
                S.op('dve', lambda e: e.tensor_scalar(out=bs.t[:, 1:2], in0=bs.t[:, 0:1], scalar1=-1.0, scalar2=None, op0=ALU.mult), reads=[bs.b], writes=[bs.b])
                S.op('dve', lambda e: e.tensor_scalar(out=bs.t[:, 4:5], in0=bs.t[:, 0:1], scalar1=2.0, scalar2=None, op0=ALU.mult), reads=[bs.b], writes=[bs.b])
                S.op('dve', lambda e: e.tensor_scalar(out=bs.t[:, 16:16 + NIT], in0=pow2.t[:, :], scalar1=bs.t[:, 4:5], scalar2=None, op0=ALU.mult),
                     reads=[bs.b, pow2.b], writes=[bs.b])
                def mask_op():
                    S.op('dve', lambda e: e.tensor_scalar(out=m_.t[:, 0:L], in0=ac.t[:, 0:L], scalar1=bs.t[:, 1:2], scalar2=NEG, op0=ALU.is_lt, op1=ALU.mult),
                         reads=[ac.b, bs.b], writes=[m_.b])
                if L <= TOPK:
                    mask_op()
                    return []
                if not act_chain:
                    S.op('dve', lambda e: e.tensor_scalar(out=bs.t[:, 16 + NIT:16 + 2 * NIT], in0=pow2.t[:, :], scalar1=bs.t[:, 4:5], scalar2=-0.5, op0=ALU.mult, op1=ALU.mult),
                         reads=[bs.b, pow2.b], writes=[bs.b])
                    S.op('dve', lambda e: e.tensor_tensor(out=bs.t[:, 1:2], in0=bs.t[:, 1:2], in1=bs.t[:, 16:17], op=ALU.add), reads=[bs.b], writes=[bs.b])
                    for it in range(NIT):
                        S.op('dve', lambda e: e.tensor_scalar(out=junkb.t[:, 0:L], in0=ac.t[:, 0:L], scalar1=bs.t[:, 1:2], scalar2=0.0, op0=ALU.is_ge, op1=ALU.add,
                                                              accum_out=bs.t[:, 6:7]),
                             reads=[ac.b, bs.b], writes=[junkb.b, bs.b])
                        S.op('dve', lambda e, it=it: e.tensor_scalar(out=bs.t[:, 7:8], in0=bs.t[:, 6:7], scalar1=TOPK - 0.5, scalar2=bs.t[:, 16 + it:17 + it],
                                                                     op0=ALU.is_gt, op1=ALU.mult),
                             reads=[bs.b], writes=[bs.b])
                        S.op('dve', lambda e, it=it: e.scalar_tensor_tensor(out=bs.t[:, 1:2], in0=bs.t[:, 1:2], scalar=bs.t[:, 16 + NIT + it:17 + NIT + it], in1=bs.t[:, 7:8],
                                                                            op0=ALU.add, op1=ALU.add),
                             reads=[bs.b], writes=[bs.b])
                    mask_op()
                    return []
                S.op('dve', lambda e: e.tensor_scalar(out=bs.t[:, 16 + NIT:16 + 2 * NIT], in0=pow2.t[:, :], scalar1=bs.t[:, 4:5], scalar2=-1.0, op0=ALU.mult, op1=ALU.mult),
                     reads=[bs.b, pow2.b], writes=[bs.b])
                S.op('dve', lambda e: e.tensor_scalar(out=bs.t[:, 16 + 2 * NIT:16 + 3 * NIT], in0=pow2.t[:, :], scalar1=bs.t[:, 4:5], scalar2=0.5, op0=ALU.mult, op1=ALU.mult),
                     reads=[bs.b, pow2.b], writes=[bs.b])
                steps = []
                for it in range(NIT):
                    def step(it=it):
                        nW = bs.t[:, 16 + NIT + it:17 + NIT + it]
                        hW = bs.t[:, 16 + 2 * NIT + it:17 + 2 * NIT + it]
                        S.op('act', lambda e: e.activation(out=bs.t[:, 5:6], in_=bs.t[:, 1:2], func=AF.Identity, scale=-1.0, bias=nW), reads=[bs.b], writes=[bs.b])
                        S.op('act', lambda e: e.activation(out=bs.t[:, 8:9], in_=bs.t[:, 1:2], func=AF.Identity, scale=1.0, bias=hW), reads=[bs.b], writes=[bs.b])
                        S.op('act', lambda e: e.activation(out=junkb2.t[:, 0:L], in_=ac.t[:, 0:L], func=AF.Sign, bias=bs.t[:, 5:6], scale=1.0, accum_out=bs.t[:, 6:7]),
                             reads=[ac.b, bs.b], writes=[junkb2.b, bs.b])
                        S.op('act', lambda e: e.activation(out=bs.t[:, 7:8], in_=bs.t[:, 6:7], func=AF.Sign, bias=float(L - 2 * (TOPK - 0.5)), scale=1.0),
                             reads=[bs.b], writes=[bs.b])
                        S.op('act', lambda e: e.activation(out=bs.t[:, 1:2], in_=bs.t[:, 7:8], func=AF.Identity, scale=hW, bias=bs.t[:, 8:9]), reads=[bs.b], writes=[bs.b])
                    steps.append(step)
                steps.append(mask_op)
                return steps

            def phaseB_Y(i, inter=()):
                inter = list(inter)
                nblk = 2 * (i + 1)
                done_blk = [0]
                m_ = mb[i % 2]
                x = xq[i % 2]
                zs = zsl[i % 2]
                S.dma('sp', x.t[:, :], x_d[i * 128:(i + 1) * 128, :], writes=[x.b])
                S.dma('sp', zs.t[:, :], zs_d[i * 128:(i + 1) * 128, :], reads=[zs_bufs[i]], writes=[zs.b])
                stb = [PB[2], PB[3], PB[0], PB[1]]
                for jb in range(i + 1):
                    lb = slice(jb * 128, (jb + 1) * 128)
                    banks = [stb[(2 * cnt_s[1]) % 4], stb[(2 * cnt_s[1] + 1) % 4]]
                    Pts = [PT[(2 * cnt_s[1]) % 6], PT[(2 * cnt_s[1] + 1) % 6]]
                    cnt_s[1] += 1
                    for g in range(2):
                        gp = slice(g * 64, (g + 1) * 64)
                        S.op('pe', lambda e, gp=gp, lb=lb, bank=banks[g]: e.matmul(bank.t[:, :], lhsT=kT_all.t[gp, lb], rhs=qT_all.t[gp, i, :], start=True, stop=False),
                             reads=[kT_all.bs[jb], qT_all.bs[i]], writes=[banks[g].b])
                    for g in range(2):
                        S.op('pe', lambda e, lb=lb, bank=banks[g]: e.matmul(bank.t[:, :], lhsT=m_.t[:, lb], rhs=ident4_b.t[:, :], start=False, stop=True),
                             reads=[m_.b, ident4_b.b], writes=[banks[g].b])
                    for g in range(2):
                        S.op('act', lambda e, bank=banks[g], Pt=Pts[g]: e.activation(out=Pt.t[:, :], in_=bank.t[:, :], func=AF.Exp), reads=[banks[g].b], writes=[Pts[g].b])
                    for g in range(2):
                        S.op('pe', lambda e, g=g, jb=jb, Pt=Pts[g]: e.matmul(PB[4 + g].t[:, :], lhsT=V_all.t[:, jb, g, :], rhs=Pt.t[:, :], start=(jb == 0), stop=(jb == i)),
                             reads=[V_all.bs[jb], Pts[g].b], writes=[PB[4 + g].b])
                    done_blk[0] += 2
                    while inter and (len(inter) > (nblk - done_blk[0]) * (NIT + 1) // nblk):
                        inter.pop(0)()
                while inter:
                    inter.pop(0)()
                for g in range(2):
                    S.op('act', lambda e, g=g: e.copy(out=OT_sb.t[:, g * 512:(g + 1) * 512], in_=PB[4 + g].t[0:65, :]), reads=[PB[4 + g].b], writes=[OT_sb.b])
                    tb = PB[6]
                    for hp in range(4):
                        S.op('pe', lambda e, g=g, hp=hp: e.transpose(out=tb.t[:, hp * 65:(hp + 1) * 65], in_=OT_sb.t[0:65, g * 512 + hp * 128:g * 512 + (hp + 1) * 128],
                                                                     identity=ident_f.t[0:65, 0:65]),
                             reads=[OT_sb.b, ident_f.b], writes=[tb.b])
                    S.op('dve', lambda e, g=g: e.reciprocal(out=rden.t[:, g * 4:(g + 1) * 4], in_=tb.t[:, 0:260].rearrange("p (h c) -> p h c", h=4)[:, :, 64]),
                         reads=[tb.b], writes=[rden.b])
                    for hp in range(4):
                        h = g * 4 + hp
                        S.op('dve', lambda e, hp=hp, h=h: e.scalar_tensor_tensor(out=zt_.t[:, h * 64:(h + 1) * 64], in0=tb.t[:, hp * 65:hp * 65 + 64],
                                                                                 scalar=rden.t[:, h:h + 1], in1=zs.t[:, h * 64:(h + 1) * 64],
                                                                                 op0=ALU.mult, op1=ALU.mult),
                             reads=[tb.b, rden.b, zs.b], writes=[zt_.b])
                tb = PB[6]
                for c in range(8):
                    src = zt_.t[:, c * 128:(c + 1) * 128] if c < 4 else zs.t[:, 512 + (c - 4) * 128:512 + (c - 3) * 128]
                    S.op('pe', lambda e, c=c, src=src: e.transpose(out=bf_view(tb)[:, c * 128:(c + 1) * 128], in_=src, identity=ident_b.t[:, :]),
                         reads=[zt_.b, zs.b, ident_b.b], writes=[tb.b])
                S.op('act', lambda e: e.copy(out=zT.t[:].rearrange("p c q -> p (c q)"), in_=bf_view(tb)), reads=[tb.b], writes=[zT.b])
                yt = yo[i % 2]
                for n in range(2):
                    bank = PB[7] if n == 0 else PB[6]
                    for c in range(8):
                        S.op('pe', lambda e, n=n, c=c, bank=bank: e.matmul(bank.t[:, :], lhsT=zT.t[:, c, :], rhs=wout_bf.t[:, c, n * 512:(n + 1) * 512],
                                                                           start=(c == 0), stop=(c == 7)),
                             reads=[zT.b, wout_bf.bs[c]], writes=[bank.b])
                    ns = slice(n * 512, (n + 1) * 512)
                    S.op('dve', lambda e, ns=ns, bank=bank: e.tensor_tensor(out=yt.t[:, ns], in0=bank.t[:, :], in1=x.t[:, ns], op=ALU.add),
                         reads=[bank.b, x.b], writes=[yt.b])
                S.dma('sp', yp_d[i * 128:(i + 1) * 128, :], yt.t[:, :], reads=[yt.b])

            if 'B' in phases:
                tl = [i for i in DBG_TILES if i < NT]
                use_act = lambda t: False
                for n, i in enumerate(tl):
                    if n == 0:
                        phaseB_X(i)
                    inter = []
                    if n + 1 < len(tl):
                        inter = phaseB_X(tl[n + 1], act_chain=use_act(tl[n + 1]))
                    phaseB_Y(i, inter)

        S.barrier()
        if 'S' in phases:
          with ExitStack() as sS:
            NV = T + NS
            wout_bf = sb([128, 8, D], BF16, 8, stack=sS)
            for k in range(8):
                S.dma('pool', wout_bf.t[:, k, :], wout_d[k * 128:(k + 1) * 128, :], writes=[wout_bf.bs[k]])
            ki_st = [sb([128, NPG, 64], F32, stack=sS) for _ in range(2)]
            kiT_c = [sb([64, 512], F32R, stack=sS) for _ in range(3)]
            Zq = sb([64, NS, 128], F32R, stack=sS)
            zero_f = sb([128, 512], F32, stack=sS)
            qiT_sall = sb([64, 128], F32R, stack=sS)
            kiT_new = sb([64, NS], F32R, stack=sS)
            R_sb = sb([128, NV], F32, stack=sS)
            Wg_a = sb([NS, 128], F32, stack=sS)
            Wg = sb([128, NS], F32, stack=sS)
            sgn_s = sb([NS, 8], F32, stack=sS)
            score = sb([NS, NV], F32, stack=sS)
            junks = sb([NS, NV], BF16, stack=sS)
            mbs = sb([NS, NV], BF16, stack=sS)
            bss = sb([NS, 16 + NIT], F32, stack=sS)
            tmps = sb([NS, NS], F32, stack=sS)
            S.op('pool', lambda e: e.memset(zero_f.t[:], 0.0), writes=[zero_f.b])
            for j in range(8):
                S.op('pe', lambda e, j=j: e.transpose(out=PB[6].t[0:64, j * 16:(j + 1) * 16], in_=qis_s.t[0:NS, j * 64:(j + 1) * 64], identity=ident_f.t[0:NS, 0:NS]),
                     reads=[qis_s.b, ident_f.b], writes=[PB[6].b])
            S.op('dve', lambda e: e.tensor_copy(out=qiT_sall.t[:, :].rearrange("d (b j) -> d j b", j=8),
                                                in_=PB[6].t[0:64, 0:128].rearrange("d (j b) -> d j b", j=8)),
                 reads=[PB[6].b], writes=[qiT_sall.b])
            for q4 in range(4):
                S.op('dve', lambda e, q4=q4: e.tensor_scalar(out=Zq.t[:, q4 * 4:(q4 + 1) * 4, :].rearrange("d b c -> d (b c)"), in0=zero_f.t[0:64, :], scalar1=0.0, scalar2=None,
                                                             op0=ALU.mult),
                     reads=[zero_f.b], writes=[Zq.b])
            for b in range(NS):
                S.op('dve', lambda e, b=b: e.tensor_copy(out=Zq.t[:, b, b * 8:(b + 1) * 8], in_=qiT_sall.t[:, b * 8:(b + 1) * 8]),
                     reads=[qiT_sall.b], writes=[Zq.b])
            S.op('pe', lambda e: e.transpose(out=PB[7].t[0:64, 0:NS], in_=ki_s.t[0:NS, :], identity=ident_f.t[0:NS, 0:NS]),
                 reads=[ki_s.b, ident_f.b], writes=[PB[7].b])
            S.op('act', lambda e: e.copy(out=kiT_new.t[:, :], in_=PB[7].t[0:64, 0:NS]), reads=[PB[7].b], writes=[kiT_new.b])
            S.op('dve', lambda e: e.tensor_scalar(out=sgn_s.t[:, :], in0=wis_s.t[:, :], scalar1=0.0, scalar2=None, op0=ALU.is_ge), reads=[wis_s.b], writes=[sgn_s.b])
            S.op('dve', lambda e: e.tensor_scalar(out=sgn_s.t[:, :], in0=sgn_s.t[:, :], scalar1=2.0, scalar2=-1.0, op0=ALU.mult, op1=ALU.add),
                 reads=[sgn_s.b], writes=[sgn_s.b])
            S.op('dve', lambda e: e.tensor_tensor(out=Wg_a.t[:, :].rearrange("p (b j) -> p b j", j=8), in0=bsel.t[:, :].rearrange("p (b j) -> p b j", j=8),
                                                  in1=sgn_s.t[:, :].unsqueeze(1).to_broadcast([NS, NS, 8]), op=ALU.mult),
                 reads=[bsel.b, sgn_s.b], writes=[Wg_a.b])
            S.op('pe', lambda e: e.transpose(out=PB[7].t[:, 32:32 + NS], in_=Wg_a.t[0:NS, :], identity=ident_f.t[0:NS, 0:NS]),
                 reads=[Wg_a.b, ident_f.b], writes=[PB[7].b])
            S.op('act', lambda e: e.copy(out=Wg.t[:, :], in_=PB[7].t[:, 32:32 + NS]), reads=[PB[7].b], writes=[Wg.b])
            cc = 0
            for b in range(NS):
                kst = ki_st[b % 2]
                S.dma('sp', kst.t[:, :, :], scrKi[:, b * NPG:(b + 1) * NPG, :], reads=[scr_bufs['ki']], writes=[kst.b])
                for ch in range(4):
                    tb = PB[6 + cc % 2]
                    kc = kiT_c[cc % 3]
                    cc += 1
                    for p4 in range(4):
                        S.op('pe', lambda e, ch=ch, p4=p4, tb=tb, kst=kst: e.transpose(out=tb.t[0:64, p4 * 128:(p4 + 1) * 128], in_=kst.t[:, ch * 4 + p4, :],
                                                                                      identity=ident_f.t[:, :]),
                             reads=[kst.b, ident_f.b], writes=[tb.b])
                    S.op('act' if cc % 2 else 'dve', (lambda e, tb=tb, kc=kc: e.copy(out=kc.t[:, :], in_=tb.t[0:64, :])) if cc % 2 else
                         (lambda e, tb=tb, kc=kc: e.tensor_copy(out=kc.t[:, :], in_=tb.t[0:64, :])),
                         reads=[tb.b], writes=[kc.b])
                    S.op('pe', lambda e, b=b, ch=ch, kc=kc: e.matmul(PB[ch].t[:, :], lhsT=Zq.t[:, b, :], rhs=kc.t[:, :], start=(b == 0), stop=(b == NS - 1)),
                         reads=[Zq.b, kc.b], writes=[PB[ch].b])
            S.op('pe', lambda e: e.matmul(PB[4].t[:, 0:NS], lhsT=qiT_sall.t[:, :], rhs=kiT_new.t[:, :], start=True, stop=True),
                 reads=[qiT_sall.b, kiT_new.b], writes=[PB[4].b])
            for ch in range(4):
                S.op('act', lambda e, ch=ch: e.activation(out=R_sb.t[:, ch * 512:(ch + 1) * 512], in_=PB[ch].t[:, :], func=AF.Relu), reads=[PB[ch].b], writes=[R_sb.b])
            S.op('act', lambda e: e.activation(out=R_sb.t[:, T:NV], in_=PB[4].t[:, 0:NS], func=AF.Relu), reads=[PB[4].b], writes=[R_sb.b])
            for ch in range(5):
                w = 512 if ch < 4 else NS
                S.op('pe', lambda e, ch=ch, w=w: e.matmul(PB[5].t[0:NS, 0:w], lhsT=Wg.t[:, :], rhs=R_sb.t[:, ch * 512:ch * 512 + w], start=True, stop=True),
                     reads=[Wg.b, R_sb.b], writes=[PB[5].b])
                S.op('act', lambda e, ch=ch, w=w: e.copy(out=score.t[:, ch * 512:ch * 512 + w], in_=PB[5].t[0:NS, 0:w]), reads=[PB[5].b], writes=[score.b])
            vg = slice(T, NV)
            S.op('dve', lambda e: e.scalar_tensor_tensor(out=tmps.t[:, :], in0=offd.t[:, :], scalar=-1.0, in1=score.t[:, vg], op0=ALU.mult, op1=ALU.add),
                 reads=[offd.b, score.b], writes=[tmps.b])
            S.op('dve', lambda e: e.tensor_tensor(out=score.t[:, vg], in0=score.t[:, vg], in1=offd.t[:, :], op=ALU.add), reads=[score.b, offd.b], writes=[score.b])
            S.op('dve', lambda e: e.tensor_reduce(out=bss.t[:, 0:1], in_=score.t[:, :], axis=AX.X, op=ALU.max), reads=[score.b], writes=[bss.b])
            S.op('dve', lambda e: e.tensor_reduce(out=bss.t[:, 1:2], in_=tmps.t[:, :], axis=AX.X, op=ALU.min), reads=[tmps.b], writes=[bss.b])
            S.op('dve', lambda e: e.tensor_reduce(out=bss.t[:, 3:4], in_=score.t[:, 0:T], axis=AX.X, op=ALU.min), reads=[score.b], writes=[bss.b])
            S.op('dve', lambda e: e.tensor_tensor(out=bss.t[:, 1:2], in0=bss.t[:, 1:2], in1=bss.t[:, 3:4], op=ALU.min), reads=[bss.b], writes=[bss.b])
            S.op('dve', lambda e: e.tensor_tensor(out=bss.t[:, 4:5], in0=bss.t[:, 0:1], in1=bss.t[:, 1:2], op=ALU.subtract), reads=[bss.b], writes=[bss.b])
            S.op('dve', lambda e: e.tensor_scalar(out=bss.t[:, 16:16 + NIT], in0=pow2.t[0:NS, :], scalar1=bss.t[:, 4:5], scalar2=None, op0=ALU.mult),
                 reads=[bss.b, pow2.b], writes=[bss.b])
            for it in range(NIT):
                S.op('dve', lambda e, it=it: e.tensor_tensor(out=bss.t[:, 5:6], in0=bss.t[:, 1:2], in1=bss.t[:, 16 + it:17 + it], op=ALU.add), reads=[bss.b], writes=[bss.b])
                S.op('dve', lambda e: e.tensor_scalar(out=junks.t[:, :], in0=score.t[:, :], scalar1=bss.t[:, 5:6], scalar2=0.0, op0=ALU.is_ge, op1=ALU.add,
                                                      accum_out=bss.t[:, 6:7]),
                     reads=[score.b, bss.b], writes=[junks.b, bss.b])
                S.op('dve', lambda e, it=it: e.tensor_scalar(out=bss.t[:, 7:8], in0=bss.t[:, 6:7], scalar1=TOPK - 0.5, scalar2=bss.t[:, 16 + it:17 + it],
                                                             op0=ALU.is_gt, op1=ALU.mult),
                     reads=[bss.b], writes=[bss.b])
                S.op('dve', lambda e: e.tensor_tensor(out=bss.t[:, 1:2], in0=bss.t[:, 1:2], in1=bss.t[:, 7:8], op=ALU.add), reads=[bss.b], writes=[bss.b])
            S.op('dve', lambda e: e.tensor_scalar(out=mbs.t[:, :], in0=score.t[:, :], scalar1=bss.t[:, 1:2], scalar2=NEG, op0=ALU.is_lt, op1=ALU.mult),
                 reads=[score.b, bss.b], writes=[mbs.b])
            Qblk = sb([128, NS, 8], BF16, stack=sS)
            kT_new = sb([128, NS], BF16, stack=sS)
            v_new = sb([NS, 128], BF16, stack=sS)
            esel = sb([NS, 128], BF16, stack=sS)
            S.op('pool', lambda e: e.memset(Qblk.t[:], 0.0), writes=[Qblk.b])
            for hp in range(4):
                S.op('pe', lambda e, hp=hp: e.transpose(out=PB[6].t[:, hp * NS:(hp + 1) * NS], in_=qn_s.t[0:NS, hp * 128:(hp + 1) * 128], identity=ident_f.t[0:NS, 0:NS]),
                     reads=[qn_s.b, ident_f.b], writes=[PB[6].b])
            for hp in range(4):
                S.op('dve', lambda e, hp=hp: e.tensor_copy(out=Qblk.t[0:64, :, hp], in_=PB[6].t[0:64, hp * NS:(hp + 1) * NS]), reads=[PB[6].b], writes=[Qblk.b])
                S.op('dve', lambda e, hp=hp: e.tensor_copy(out=Qblk.t[64:128, :, 4 + hp], in_=PB[6].t[64:128, hp * NS:(hp + 1) * NS]), reads=[PB[6].b], writes=[Qblk.b])
            S.op('pe', lambda e: e.transpose(out=PB[7].t[:, 0:NS], in_=kn_s.t[0:NS, :], identity=ident_f.t[0:NS, 0:NS]), reads=[kn_s.b, ident_f.b], writes=[PB[7].b])
            S.op('act', lambda e: e.copy(out=kT_new.t[:, :], in_=PB[7].t[:, 0:NS]), reads=[PB[7].b], writes=[kT_new.b])
            S.op('dve', lambda e: e.tensor_copy(out=v_new.t[:, :], in_=v_s.t[:, :]), reads=[v_s.b], writes=[v_new.b])
            S.op('dve', lambda e: e.tensor_copy(out=esel.t[:, :], in_=bsel.t[:, :]), reads=[bsel.b], writes=[esel.b])
            k_st = [sb([128, NPG, 128], BF16, stack=sS) for _ in range(2)]
            v_st = [sb([128, NPG, 128], BF16, stack=sS) for _ in range(2)]
            kT_sb = [sb([128, T], BF16, stack=sS) for _ in range(2)]
            PTs = [sb([128, 136], BF16, stack=sS) for _ in range(2)]
            dn_all = sb([128, 128], F32, stack=sS)

            def p2_load(b):
                kst, vst = k_st[b % 2], v_st[b % 2]
                S.dma('pool', kst.t[:, :, :], scrK[:, b * NPG:(b + 1) * NPG, :], reads=[scr_bufs['k']], writes=[kst.b])
                S.dma('pool', vst.t[:, :, :], scrV[:, b * NPG:(b + 1) * NPG, :], reads=[scr_bufs['v']], writes=[vst.b])

            def p2_T(b):
                kst, kT = k_st[b % 2], kT_sb[b % 2]
                for hf in range(2):
                    tb = PB[hf]
                    for p8 in range(8):
                        S.op('pe', lambda e, hf=hf, p8=p8, tb=tb, kst=kst: e.transpose(out=bf_view(tb)[:, p8 * 128:(p8 + 1) * 128], in_=kst.t[:, hf * 8 + p8, :],
                                                                                      identity=ident_b.t[:, :]),
                             reads=[kst.b, ident_b.b], writes=[tb.b])
                    S.op('act' if hf else 'dve', (lambda e, tb=tb, kT=kT, hf=hf: e.copy(out=kT.t[:, hf * 1024:(hf + 1) * 1024], in_=bf_view(tb))) if hf else
                         (lambda e, tb=tb, kT=kT, hf=hf: e.tensor_copy(out=kT.t[:, hf * 1024:(hf + 1) * 1024], in_=bf_view(tb))),
                         reads=[tb.b], writes=[kT.b])

            def p2_L(b):
                kT = kT_sb[b % 2]
                LT = PB[2 + b % 2]
                Pt = PTs[b % 2]
                es = esel.t[:, b * 8:(b + 1) * 8]
                for jj in range(NPG):
                    S.op('pe', lambda e, jj=jj, LT=LT, kT=kT, b=b: e.matmul(LT.t[:, jj * 8:(jj + 1) * 8], lhsT=kT.t[:, jj * 128:(jj + 1) * 128], rhs=Qblk.t[:, b, :],
                                                                           start=True, stop=False),
                         reads=[kT.b, Qblk.b], writes=[LT.b])
                    S.op('pe', lambda e, jj=jj, LT=LT, es=es: e.matmul(LT.t[:, jj * 8:(jj + 1) * 8], lhsT=mbs.t[:, jj * 128:(jj + 1) * 128], rhs=es, start=False, stop=True),
                         reads=[mbs.b, esel.b], writes=[LT.b])
                S.op('pe', lambda e, LT=LT, b=b: e.matmul(LT.t[0:NS, 128:136], lhsT=kT_new.t[:, :], rhs=Qblk.t[:, b, :], start=True, stop=False),
                     reads=[kT_new.b, Qblk.b], writes=[LT.b])
                S.op('pe', lambda e, LT=LT, es=es: e.matmul(LT.t[0:NS, 128:136], lhsT=mbs.t[:, vg], rhs=es, start=False, stop=True),
                     reads=[mbs.b, esel.b], writes=[LT.b])
                S.op('act', lambda e, LT=LT, Pt=Pt: e.activation(out=Pt.t[:, 0:128], in_=LT.t[:, 0:128], func=AF.Exp), reads=[LT.b], writes=[Pt.b])
                S.op('act', lambda e, LT=LT, Pt=Pt: e.activation(out=Pt.t[0:NS, 128:136], in_=LT.t[0:NS, 128:136], func=AF.Exp), reads=[LT.b], writes=[Pt.b])

            def p2_PV(b):
                vst = v_st[b % 2]
                Pt = PTs[b % 2]
                oc = slice(b * 8, (b + 1) * 8)
                for jj in range(NPG):
                    S.op('pe', lambda e, jj=jj, oc=oc, Pt=Pt, vst=vst: e.matmul(PB[4].t[:, oc], lhsT=vst.t[:, jj, :], rhs=Pt.t[:, jj * 8:(jj + 1) * 8], start=(jj == 0), stop=False),
                         reads=[vst.b, Pt.b], writes=[PB[4].b])
                S.op('pe', lambda e, oc=oc, Pt=Pt: e.matmul(PB[4].t[:, oc], lhsT=v_new.t[:, :], rhs=Pt.t[0:NS, 128:136], start=False, stop=True),
                     reads=[v_new.b, Pt.b], writes=[PB[4].b])
                S.op('pe', lambda e, Pt=Pt: e.matmul(PB[5].t[:, 0:128], lhsT=ones_b.t[:, :], rhs=Pt.t[:, 0:128], start=True, stop=True),
                     reads=[ones_b.b, Pt.b], writes=[PB[5].b])
                S.op('pe', lambda e, Pt=Pt: e.matmul(PB[5].t[:, 128:136], lhsT=ones_b.t[0:NS, :], rhs=Pt.t[0:NS, 128:136], start=True, stop=True),
                     reads=[ones_b.b, Pt.b], writes=[PB[5].b])
                S.op('dve', lambda e, oc=oc: e.tensor_reduce(out=dn_all.t[:, oc], in_=PB[5].t[:, 0:136].rearrange("p (j h) -> p h j", h=8), axis=AX.X, op=ALU.add),
                     reads=[PB[5].b], writes=[dn_all.b])

            p2_load(0)
            p2_T(0)
            for b in range(NS):
                if b + 1 < NS:
                    p2_load(b + 1)
                p2_L(b)
                if b + 1 < NS:
                    p2_T(b + 1)
                p2_PV(b)
            rdn = sb([128, 128], F32, stack=sS)
            z_s = sb([NS, D], BF16, stack=sS)
            zT_s = sb([128, 8, NS], BF16, stack=sS)
            class _V:
                pass
            xs_sb = _V(); xs_sb.t = R_sb.t[0:NS, 0:D]; xs_sb.b = R_sb.b
            ys_sb = _V(); ys_sb.t = R_sb.t[0:NS, D:2 * D]; ys_sb.b = R_sb.b
            S.dma('sp', xs_sb.t[:, :], xs_d, writes=[xs_sb.b])
            S.op('dve', lambda e: e.reciprocal(out=rdn.t[:, :], in_=dn_all.t[:, :]), reads=[dn_all.b], writes=[rdn.b])
            aTg = [sb([64, 8, NS], F32, stack=sS) for _ in range(2)]
            for g in range(2):
                gp = slice(g * 64, (g + 1) * 64)
                S.op('dve', lambda e, g=g, gp=gp: e.tensor_tensor(out=aTg[g].t[:, :, :], in0=PB[4].t[gp, 0:128].rearrange("p (b h) -> p h b", h=8),
                                                                  in1=rdn.t[gp, :].rearrange("p (b h) -> p h b", h=8), op=ALU.mult),
                     reads=[PB[4].b, rdn.b], writes=[aTg[g].b])
            for h in range(8):
                g = h // 4
                S.op('pe', lambda e, h=h, g=g: e.transpose(out=PB[6].t[0:NS, h * 64:(h + 1) * 64], in_=aTg[g].t[:, h, :], identity=ident_f.t[0:64, 0:64]),
                     reads=[aTg[g].b, ident_f.b], writes=[PB[6].b])
            S.op('dve', lambda e: e.tensor_tensor(out=z_s.t[:, 0:512], in0=PB[6].t[0:NS, :], in1=sga_s.t[:, :], op=ALU.mult), reads=[PB[6].b, sga_s.b], writes=[z_s.b])
            S.op('dve', lambda e: e.tensor_copy(out=z_s.t[:, 512:1024], in_=zp_s.t[:, :]), reads=[zp_s.b], writes=[z_s.b])
            for c in range(8):
                S.op('pe', lambda e, c=c: e.transpose(out=bf_view(PB[7])[:, c * NS:(c + 1) * NS], in_=z_s.t[0:NS, c * 128:(c + 1) * 128], identity=ident_b.t[0:NS, 0:NS]),
                     reads=[z_s.b, ident_b.b], writes=[PB[7].b])
            S.op('act', lambda e: e.copy(out=zT_s.t[:].rearrange("p c b -> p (c b)"), in_=bf_view(PB[7])[:, 0:8 * NS]), reads=[PB[7].b], writes=[zT_s.b])
            for n in range(2):
                for c in range(8):
                    S.op('pe', lambda e, n=n, c=c: e.matmul(PB[n].t[0:NS, :], lhsT=zT_s.t[:, c, :], rhs=wout_bf.t[:, c, n * 512:(n + 1) * 512], start=(c == 0), stop=(c == 7)),
                         reads=[zT_s.b, wout_bf.bs[c]], writes=[PB[n].b])
                ns = slice(n * 512, (n + 1) * 512)
                S.op('dve', lambda e, n=n, ns=ns: e.tensor_tensor(out=ys_sb.t[:, ns], in0=PB[n].t[0:NS, :], in1=ada_sb.t[0:NS, 2 * D + n * 512:2 * D + (n + 1) * 512], op=ALU.mult),
                     reads=[PB[n].b, ada_sb.b], writes=[ys_sb.b])
                S.op('dve', lambda e, ns=ns: e.tensor_tensor(out=ys_sb.t[:, ns], in0=ys_sb.t[:, ns], in1=xs_sb.t[:, ns], op=ALU.add), reads=[ys_sb.b, xs_sb.b], writes=[ys_sb.b])
            S.dma('sp', ys_d, ys_sb.t[:, :], reads=[ys_sb.b])

        S.build()
        print("total ops", S.nops)
    return nc


_CACHE = {}


PHASES = '0ASB'
DBG_TILES = list(range(NT + 1))


def _get_program():
    if 'nc' not in _CACHE:
        _CACHE['nc'] = build_program(PHASES)
    return _CACHE['nc']


def kernel(x_prompt, x_sample, cache_k, cache_v, cache_kidx, state_pool, page_table, c_prompt, c_sample,
           norm_w, w_ada, b_ada, w_in, q_norm_w, k_norm_w, w_pool, pool_scale, w_out):
    f32 = np.float32
    nc = _get_program()
    consts = _consts()
    perm = _win_perm()
    win_p = np.ascontiguousarray(np.asarray(w_in, f32)[0][:, perm])
    ck = np.ascontiguousarray(np.asarray(cache_k, f32)[0].reshape(2560 * 128, 128))
    cv = np.ascontiguousarray(np.asarray(cache_v, f32)[0].reshape(2560 * 128, 128))
    cki = np.ascontiguousarray(np.asarray(cache_kidx, f32)[0].reshape(2560 * 128, 64))
    shared = dict(norm_w=np.asarray(norm_w, f32).reshape(1, D), w_ada=np.asarray(w_ada, f32)[0],
                  b_ada=np.asarray(b_ada, f32).reshape(1, 3 * D), w_in=win_p,
                  q_norm_w=np.asarray(q_norm_w, f32).reshape(1, 64), k_norm_w=np.asarray(k_norm_w, f32).reshape(1, 64),
                  w_pool=np.asarray(w_pool, f32)[0], pool_scale=np.asarray(pool_scale, f32).reshape(1, 512),
                  w_out=np.asarray(w_out, f32)[0], **consts)
    if 'S' in PHASES:
        shared.update(cache_k=ck, cache_v=cv, cache_ki=cki)
    in_maps = []
    for c in range(8):
        cvec = np.zeros((33, D), f32)
        cvec[0:16] = np.asarray(c_sample, f32)[c * 16:(c + 1) * 16]
        cvec[32] = np.asarray(c_prompt, f32)[c]
        m = dict(shared)
        m.update(x=np.ascontiguousarray(np.asarray(x_prompt, f32)[c]),
                 xs=np.ascontiguousarray(np.asarray(x_sample, f32)[c * 16:(c + 1) * 16, 0, :]),
                 cvec=cvec,
                 state_pool=np.ascontiguousarray(np.asarray(state_pool, f32)[0, c * 16:(c + 1) * 16].reshape(16, 15 * 512)),
                 page_table=np.ascontiguousarray(np.asarray(page_table, np.int32)[c * 16:(c + 1) * 16].reshape(1, 256)))
        in_maps.append(m)
    res = run_bass_kernel_spmd(nc, in_maps, core_ids=list(range(8)))
    r = res.results
    _CACHE['last'] = r
    cat = lambda k: np.stack([np.asarray(r[c][k], f32) for c in range(8)], 0)
    y_p = cat('y_p')
    y_s = cat('y_s').reshape(128, 1, D)
    k_p = cat('k_p').reshape(1, 8, T, 2, 64)
    v_p = cat('v_p').reshape(1, 8, T, 2, 64)
    ki_p = cat('ki_p').reshape(1, 8, T, 64)
    pool_p = cat('pool_p').reshape(1, 8, 15, 512)
    k_s = cat('k_s').reshape(1, 128, 1, 2, 64)
    v_s = cat('v_s').reshape(1, 128, 1, 2, 64)
    ki_s = cat('ki_s').reshape(1, 128, 1, 64)
    pool_s = cat('pool_s').reshape(1, 128, 15, 512)
    return (y_p, y_s, k_p, v_p, ki_p, pool_p, k_s, v_s, ki_s, pool_s)
```

```python
from contextlib import ExitStack
import numpy as np
import concourse.bass as bass
import concourse.mybir as mybir
from concourse.bass_utils import run_bass_kernel_spmd

F32 = mybir.dt.float32
F32R = mybir.dt.float32r
BF16 = mybir.dt.bfloat16
I32 = mybir.dt.int32
ALU = mybir.AluOpType
AF = mybir.ActivationFunctionType
AX = mybir.AxisListType

ENGS = ['pe', 'act', 'dve', 'pool', 'sp']
DBG_MAXOPS = 10 ** 9
DBG_DUMP = False

D = 1024
T = 2048
NT = 16
NS = 16
NPG = 16
NIN = 2888
EPS = 1e-6
NEG = -30000.0
BIGNEG = -1.0e30
NIT = 10
TOPK = 256


class Buf:
    __slots__ = ('name', 'w', 'r', 'excl')

    def __init__(self, name='', excl=False):
        self.name = name
        self.w = None
        self.r = []
        self.excl = excl


class Sched:
    def __init__(self, nc, stack, n_dma_sems=48):
        self.nc = nc
        self.q = {e: [] for e in ENGS}
        self.tick = {e: 0 for e in ENGS}
        self.seen = {e: {} for e in ENGS}
        self.esem = {e: stack.enter_context(nc.semaphore('es_' + e)) for e in ENGS}
        self.dsem = [stack.enter_context(nc.semaphore('ds%d' % i)) for i in range(n_dma_sems)]
        self.dcnt = [0] * n_dma_sems
        self.dnext = 0
        self.dnext_q = {}
        self.barrier_events = []

    def _collect(self, eng, reads, writes):
        evs = []
        for b in reads:
            if b.w is not None:
                evs.append(b.w)
            if b.excl:
                evs.extend(x for x in b.r if not (x[0] == 'eng' and x[1] == eng))
        for b in writes:
            if b.w is not None:
                evs.append(b.w)
            evs.extend(b.r)
        seen = self.seen[eng]
        m = {}
        for ev in evs:
            if ev[0] == 'eng':
                _, e, t = ev
                if e == eng and eng == 'pe':
                    continue
                key = ('eng', e)
                sem = self.esem[e]
            else:
                _, s, t = ev
                key = ('dma', s)
                sem = self.dsem[s]
            if seen.get(key, 0) >= t:
                continue
            if key not in m or m[key][1] < t:
                m[key] = (sem, t)
        for key, (sem, t) in m.items():
            seen[key] = t
        return list(m.values())

    def _record(self, ev, reads, writes):
        for b in reads:
            if ev[0] == 'eng':
                b.r = [x for x in b.r if not (x[0] == 'eng' and x[1] == ev[1])]
            b.r.append(ev)
        for b in writes:
            b.w = ev
            b.r = []

    def op(self, eng, fn, reads=(), writes=()):
        self.nops = getattr(self, 'nops', 0) + 1
        if self.nops > DBG_MAXOPS:
            return
        waits = self._collect(eng, reads, writes)
        self.tick[eng] += 1
        t = self.tick[eng]
        sem = self.esem[eng]

        def run(e, waits=waits, fn=fn, sem=sem):
            for s, v in waits:
                e.wait_ge(s, v)
            fn(e).then_inc(sem, 1)
        self.q[eng].append(run)
        self._record(('eng', eng, t), reads, writes)

    def dma(self, queue, out, in_, reads=(), writes=(), fn=None, **kw):
        self.nops = getattr(self, 'nops', 0) + 1
        if self.nops > DBG_MAXOPS:
            return
        waits = self._collect(queue, reads, writes)
        lo_, hi_ = (0, len(self.dsem) // 2) if queue == 'sp' else (len(self.dsem) // 2, len(self.dsem))
        nx = self.dnext_q.get(queue, lo_)
        s = nx
        self.dnext_q[queue] = lo_ + (nx + 1 - lo_) % (hi_ - lo_)
        prev = self.dcnt[s] * 16
        self.dcnt[s] += 1
        v = self.dcnt[s] * 16
        sem = self.dsem[s]
        if prev > 0 and self.seen[queue].get(('dma', s), 0) < prev:
            self.seen[queue][('dma', s)] = prev
            waits = [w for w in waits if w[0] is not sem] + [(sem, prev)]

        def run(e, waits=waits, sem=sem, out=out, in_=in_, kw=kw, fn=fn):
            for s_, v_ in waits:
                e.wait_ge(s_, v_)
            if fn is not None:
                fn(e).then_inc(sem, 16)
            else:
                e.dma_start(out=out, in_=in_, **kw).then_inc(sem, 16)
        self.q[queue].append(run)
        self._record(('dma', s, v), reads, writes)

    def barrier(self):
        ev = [('eng', e, self.tick[e]) for e in ENGS if self.tick[e] > 0]
        ev += [('dma', s, c * 16) for s, c in enumerate(self.dcnt) if c > 0 and s < len(self.dsem) // 2]
        self.barrier_events = ev

    def build(self):
        nc = self.nc
        fin = [(self.dsem[s], c * 16) for s, c in enumerate(self.dcnt) if c > 0]
        fin += [(self.esem[e], self.tick[e]) for e in ENGS if self.tick[e] > 0]

        def last(e, fin=fin):
            for s_, v_ in fin:
                e.wait_ge(s_, v_)
        self.q['sp'].append(last)
        with nc.Block() as block:
            @block.sync
            def _(e):
                for f in self.q['sp']:
                    f(e)

            @block.tensor
            def _(e):
                for f in self.q['pe']:
                    f(e)

            @block.scalar
            def _(e):
                for f in self.q['act']:
                    f(e)

            @block.vector
            def _(e):
                for f in self.q['dve']:
                    f(e)

            @block.gpsimd
            def _(e):
                for f in self.q['pool']:
                    f(e)


class TL:
    def __init__(self, t, nb=1, name='', excl=False, init_r=()):
        self.t = t
        self.bs = [Buf(name + str(i), excl) for i in range(nb)]
        for b in self.bs:
            b.r = list(init_r)

    @property
    def b(self):
        return self.bs[0]


def _pool_mats():
    wins = (2, 4, 8, 16)
    m0 = np.zeros((4, 128, 128), np.float32)
    mc = np.zeros((4, 128, 128), np.float32)
    mp = np.zeros((4, 128, 128), np.float32)
    tp = np.arange(128)[:, None]
    t = np.arange(128)[None, :]
    for g, w in enumerate(wins):
        band = ((t - tp) >= 0) & ((t - tp) < w)
        cnt0 = np.minimum(t + 1, w).astype(np.float32)
        m0[g] = band / cnt0 - (t == tp)
        mc[g] = band / np.float32(w) - (t == tp)
        bandp = ((t + 128 - tp) >= 0) & ((t + 128 - tp) < w)
        mp[g] = bandp / np.float32(w)
    return m0, mc, mp


def _consts():
    m0, mc, mp = _pool_mats()
    c = {}
    c['pmat'] = np.ascontiguousarray(np.stack([m0, mc, mp], 0).transpose(2, 0, 1, 3)).reshape(128, 3 * 4 * 128)
    ident = np.eye(128, dtype=np.float32)
    c['ident'] = ident
    c['ident4'] = np.concatenate([ident] * 4, axis=1)
    tri = np.where(np.arange(128)[None, :] <= np.arange(128)[:, None], 0.0, BIGNEG).astype(np.float32)
    c['causal'] = tri
    offd = np.where(np.eye(16) > 0, 0.0, BIGNEG).astype(np.float32)
    c['offdiag'] = offd
    c['bsel'] = np.repeat(np.eye(16, dtype=np.float32), 8, axis=1)
    c['pow2'] = (2.0 ** -(np.arange(NIT, dtype=np.float32) + 1.0))[None, :].repeat(128, 0).astype(np.float32)
    return c


def _win_perm():
    segs = {}
    off = 0
    for name, w in (('q', 512), ('k', 128), ('v', 128), ('qi', 512), ('ki', 64), ('wi', 8),
                    ('ga', 512), ('u', 512), ('gp', 512)):
        segs[name] = np.arange(off, off + w)
        off += w
    hord = [g * 4 + hp for hp in range(4) for g in range(2)]
    for n in ('q', 'qi'):
        segs[n] = np.concatenate([segs[n][h * 64:(h + 1) * 64] for h in hord])
    segs['wi'] = segs['wi'][hord]
    order = ['q', 'qi', 'ga', 'gp', 'u', 'k', 'v', 'ki', 'wi']
    return np.concatenate([segs[n] for n in order])


def build_program(phases='0ASB'):
    nc = bass.Bass("TRN2", target_bir_lowering=False)
    din = lambda n, s, d=F32: nc.dram_tensor(n, s, d, kind="ExternalInput").ap()
    dout = lambda n, s, d=F32: nc.dram_tensor(n, s, d, kind="ExternalOutput").ap()
    x_d = din("x", [T, D])
    xs_d = din("xs", [NS, D])
    cv_d = din("cvec", [33, D])
    if 'S' in phases:
        ck_d = din("cache_k", [2560 * 128, 128])
        cvv_d = din("cache_v", [2560 * 128, 128])
        cki_d = din("cache_ki", [2560 * 128, 64])
    sp_d = din("state_pool", [NS, 15 * 512])
    pt_d = din("page_table", [1, NS * NPG], I32)
    nw_d = din("norm_w", [1, D])
    wada_d = din("w_ada", [D, 3 * D])
    bada_d = din("b_ada", [1, 3 * D])
    win_d = din("w_in", [D, NIN])
    qw_d = din("q_norm_w", [1, 64])
    kw_d = din("k_norm_w", [1, 64])
    wpool_d = din("w_pool", [4, 128, 128])
    psc_d = din("pool_scale", [1, 512])
    wout_d = din("w_out", [D, D])
    pmat_d = din("pmat", [128, 1536])
    ident_d = din("ident", [128, 128])
    ident4_d = din("ident4", [128, 512])
    causal_d = din("causal", [128, 128])
    offd_d = din("offdiag", [16, 16])
    bsel_d = din("bsel", [16, 128])
    pow2_d = din("pow2", [128, NIT])

    yp_d = dout("y_p", [T, D])
    ys_d = dout("y_s", [NS, D])
    kp_d = dout("k_p", [T, 128])
    vp_d = dout("v_p", [T, 128])
    kip_d = dout("ki_p", [T, 64])
    pp_d = dout("pool_p", [15, 512])
    ks_d = dout("k_s", [NS, 128])
    vs_d = dout("v_s", [NS, 128])
    kis_d = dout("ki_s", [NS, 64])
    ps_d = dout("pool_s", [NS, 15 * 512])

    dbg = {}
    if DBG_DUMP:
        dbg['ada'] = dout("dbg_ada", [33, 3 * D])
        dbg['A_p'] = dout("dbg_A_p", [128, D])
        dbg['B_p'] = dout("dbg_B_p", [128, D])
        dbg['h1'] = dout("dbg_h1", [128, D])
        dbg['hb'] = dout("dbg_hb", [128, D])
        dbg['hT'] = dout("dbg_hT", [128, 1024])
        dbg['win'] = dout("dbg_win", [128, NIN])
        dbg['stt'] = dout("dbg_stt", [128, 40])
        dbg['nw'] = dout("dbg_nw", [128, D])
        dbg['sc'] = dout("dbg_sc", [128, 512])
    with ExitStack() as st:
        S = Sched(nc, st)
        _cnt = [0]

        def sb(shape, dt, nb=1, name=None, stack=st):
            _cnt[0] += 1
            name = name or ('t%d' % _cnt[0])
            return TL(stack.enter_context(nc.sbuf_tensor(name, list(shape), dt)), nb, name, init_r=S.barrier_events)

        def pst(shape, dt, nb=1, name=None, stack=st):
            _cnt[0] += 1
            name = name or ('p%d' % _cnt[0])
            return TL(stack.enter_context(nc.psum_tensor(name, list(shape), dt)), nb, name, excl=True)

        PB = [pst([128, 512], F32, name='bank%d' % i) for i in range(8)]

        def bf_view(bank):
            return bank.t[:].bitcast(BF16)

        ident_f = sb([128, 128], F32)
        ident_b = sb([128, 128], BF16)
        ident4_b = sb([128, 512], BF16)
        causal = sb([128, 128], F32)
        offd = sb([16, 16], F32)
        bsel = sb([16, 128], F32)
        pow2 = sb([128, NIT], F32)
        ones_f = sb([128, 128], F32)
        ones_b = sb([128, 128], BF16)
        qw_bc = sb([128, 512], F32)
        kw_bc = sb([128, 128], F32)
        ada_sb = sb([33, 3 * D], F32)
        s0A = st.enter_context(ExitStack())
        A_p = sb([128, D], F32, stack=s0A)
        B_p = sb([128, D], F32, stack=s0A)
        A_s = sb([NS, D], F32, stack=s0A)
        eps_t = sb([128, 1], F32)

        S.dma('sp', ident_f.t[:], ident_d, writes=[ident_f.b])
        S.dma('pool', ident_b.t[:], ident_d, writes=[ident_b.b])
        S.dma('pool', ident4_b.t[:], ident4_d, writes=[ident4_b.b])
        S.dma('sp', causal.t[:], causal_d, writes=[causal.b])
        S.dma('sp', offd.t[:], offd_d, writes=[offd.b])
        S.dma('sp', bsel.t[:], bsel_d, writes=[bsel.b])
        S.dma('sp', pow2.t[:], pow2_d, writes=[pow2.b])
        for h in range(8):
            S.dma('sp', qw_bc.t[:, h * 64:(h + 1) * 64], qw_d.partition_broadcast(128), writes=[qw_bc.b])
        for g in range(2):
            S.dma('sp', kw_bc.t[:, g * 64:(g + 1) * 64], kw_d.partition_broadcast(128), writes=[kw_bc.b])
        S.op('dve', lambda e: e.memset(ones_f.t[:], 1.0), writes=[ones_f.b])
        S.op('dve', lambda e: e.memset(ones_b.t[:], 1.0), writes=[ones_b.b])
        S.op('dve', lambda e: e.memset(eps_t.t[:], EPS), writes=[eps_t.b])
        S.op('dve', lambda e: e.tensor_scalar(out=qw_bc.t[:], in0=qw_bc.t[:], scalar1=0.125, scalar2=None, op0=ALU.mult),
             reads=[qw_bc.b], writes=[qw_bc.b])

        qT_all = sb([128, NT, 512], BF16, NT)
        qiT_all = sb([128, NT, 512], F32R, NT)
        kT_all = sb([128, T], BF16, NT)
        kiT_all = sb([128, T], F32R, NT)
        V_all = sb([128, NT, 2, 128], BF16, NT)
        zs_d = nc.dram_tensor("zs_scratch", [T, 1024], BF16, kind="Internal").ap()
        zs_bufs = [Buf('zs%d' % i) for i in range(NT)]
        wis_all = sb([128, NT, 8], F32, NT)
        sgn_all = sb([128, NT, 8], F32, NT)
        qn_s = sb([NS, 512], F32)
        qis_s = sb([NS, 512], F32)
        kn_s = sb([NS, 128], F32)
        v_s = sb([NS, 128], F32)
        ki_s = sb([NS, 64], F32)
        wis_s = sb([NS, 8], F32)
        sga_s = sb([NS, 512], BF16)
        zp_s = sb([NS, 512], BF16)

        S.op('pool', lambda e: e.memset(V_all.t[:, :, :, 64:128], 0.0), writes=V_all.bs)
        S.op('pool', lambda e: e.memset(V_all.t[:, :, :, 64:65], 1.0), writes=V_all.bs)

        scr_bufs = {}
        if 'S' in phases:
            scrK = nc.dram_tensor("scr_k", [128, NS * NPG, 128], F32, kind="Internal").ap()
            scrV = nc.dram_tensor("scr_v", [128, NS * NPG, 128], F32, kind="Internal").ap()
            scrKi = nc.dram_tensor("scr_ki", [128, NS * NPG, 64], F32, kind="Internal").ap()
            GW = 512
            ptc = sb([128, 2], I32)
            ptf = sb([128, 2], F32)
            idxg = sb([128, 2, 32], I32)
            idxi = sb([128, 2, 16], I32)
            qoff = sb([128, 32], F32)
            qoff_i = sb([128, 32], I32)
            stg = [sb([128, GW], F32) for _ in range(2)]
            ptf8 = sb([128, 4], F32)
        sW = ExitStack()
        win_bf = sb([128, 8, NIN], BF16, 8, stack=sW)
        pmat = sb([128, 3, 4, 128], BF16, stack=sW)
        sWa = ExitStack()
        wa = [sb([128, 3 * D], BF16, stack=sWa) for _ in range(2)]
        for k in range(2):
            for hf in range(2):
                S.dma('pool', wa[k].t[:, hf * 1536:(hf + 1) * 1536], wada_d[k * 128:(k + 1) * 128, hf * 1536:(hf + 1) * 1536], writes=[wa[k].b])
        with ExitStack() as s0:
            c_sb = sb([33, D], F32, stack=s0)
            normw_bc = sb([128, D], F32, stack=s0)
            S.dma('sp', normw_bc.t[:], nw_d.partition_broadcast(128), writes=[normw_bc.b])
            sc = sb([33, D], F32, stack=s0)
            scT = sb([128, 8, 33], BF16, stack=s0)
            bada_bc = sb([33, 3 * D], F32, stack=s0)
            S.dma('sp', c_sb.t[:], cv_d, writes=[c_sb.b])
            S.dma('sp', bada_bc.t[:], bada_d.partition_broadcast(33), writes=[bada_bc.b])
            S.op('act', lambda e: e.activation(out=sc.t[:], in_=c_sb.t[:], func=AF.Silu), reads=[c_sb.b], writes=[sc.b])
            for k in range(8):
                S.op('pe', lambda e, k=k: e.transpose(out=PB[7].t[:, k * 33:(k + 1) * 33], in_=sc.t[0:33, k * 128:(k + 1) * 128],
                                                      identity=ident_f.t[0:33, 0:33]),
                     reads=[sc.b, ident_f.b], writes=[PB[7].b])
            S.op('dve', lambda e: e.tensor_copy(out=scT.t[:].rearrange("p k m -> p (k m)"), in_=PB[7].t[:, 0:8 * 33]),
                 reads=[PB[7].b], writes=[scT.b])
            for k in range(8):
                w = wa[k % 2]
                if k >= 2:
                    for hf in range(2):
                        S.dma('pool', w.t[:, hf * 1536:(hf + 1) * 1536], wada_d[k * 128:(k + 1) * 128, hf * 1536:(hf + 1) * 1536], writes=[w.b])
                for n in range(6):
                    S.op('pe', lambda e, k=k, n=n, w=w: e.matmul(PB[n].t[0:33, :], lhsT=scT.t[:, k, :], rhs=w.t[:, n * 512:(n + 1) * 512],
                                                                 start=(k == 0), stop=(k == 7)),
                         reads=[scT.b, w.b], writes=[PB[n].b])
            for k in range(8):
                for hf in range(2):
                    c0, c1 = hf * 1444, (hf + 1) * 1444
                    S.dma('pool', win_bf.t[:, k, c0:c1], win_d[k * 128:(k + 1) * 128, c0:c1], writes=[win_bf.bs[k]])
            S.dma('pool', pmat.t[:].rearrange("p a g t -> p (a g t)"), pmat_d, writes=[pmat.b])
            for n in range(6):
                S.op('dve', lambda e, n=n: e.tensor_tensor(out=ada_sb.t[:, n * 512:(n + 1) * 512], in0=PB[n].t[0:33, :],
                                                           in1=bada_bc.t[:, n * 512:(n + 1) * 512], op=ALU.add),
                     reads=[PB[n].b, bada_bc.b], writes=[ada_sb.b])
            for n in range(4):
                S.op('pe', lambda e, n=n: e.matmul(PB[n].t[:, :], lhsT=ones_f.t[32:33, 0:128], rhs=ada_sb.t[32:33, n * 512:(n + 1) * 512],
                                                   start=True, stop=True),
                     reads=[ones_f.b, ada_sb.b], writes=[PB[n].b])
            if DBG_DUMP:
                S.dma('sp', dbg['nw'], normw_bc.t[:], reads=[normw_bc.b])
                S.op('act', lambda e: e.copy(out=sc.t[0:33, 0:512], in_=PB[2].t[0:33, :]), reads=[PB[2].b], writes=[sc.b])
                S.dma('sp', dbg['sc'][0:33, :], sc.t[0:33, 0:512], reads=[sc.b])
            for hf in range(2):
                cs = slice(hf * 512, (hf + 1) * 512)
                S.op('act', lambda e, hf=hf, cs=cs: e.copy(out=B_p.t[:, cs], in_=PB[hf].t[:, :]), reads=[PB[hf].b], writes=[B_p.b])
                S.op('dve', lambda e, hf=hf, cs=cs: e.scalar_tensor_tensor(out=A_p.t[:, cs], in0=PB[2 + hf].t[:, :], scalar=1.0, in1=normw_bc.t[:, cs],
                                                                           op0=ALU.add, op1=ALU.mult),
                     reads=[PB[2 + hf].b, normw_bc.b], writes=[A_p.b])
            S.op('dve', lambda e: e.scalar_tensor_tensor(out=A_s.t[:, :], in0=ada_sb.t[0:NS, D:2 * D], scalar=1.0, in1=normw_bc.t[0:NS, :],
                                                         op0=ALU.add, op1=ALU.mult),
                 reads=[ada_sb.b, normw_bc.b], writes=[A_s.b])

        if DBG_DUMP:
            S.dma('sp', dbg['ada'], ada_sb.t[:], reads=[ada_sb.b])
            S.dma('sp', dbg['A_p'], A_p.t[:], reads=[A_p.b])
            S.dma('sp', dbg['B_p'], B_p.t[:], reads=[B_p.b])
        sWa.close()
        S.barrier()
        if 'S' in phases:
            for hf in range(2):
                S.dma('sp', ptc.t[:, hf:hf + 1], pt_d[:, hf * 128:(hf + 1) * 128].rearrange("o p -> p o"), writes=[ptc.b])
            S.op('pool', lambda e: e.iota(out=qoff_i.t[:], pattern=[[1, 32]], base=0, channel_multiplier=0), writes=[qoff_i.b])
            S.op('dve', lambda e: e.tensor_copy(out=qoff.t[:], in_=qoff_i.t[:]), reads=[qoff_i.b], writes=[qoff.b])
            S.op('dve', lambda e: e.tensor_copy(out=ptf.t[:], in_=ptc.t[:]), reads=[ptc.b], writes=[ptf.b])
            S.op('dve', lambda e: e.tensor_scalar(out=ptf8.t[:, 0:2], in0=ptf.t[:, :], scalar1=32.0, scalar2=None, op0=ALU.mult), reads=[ptf.b], writes=[ptf8.b])
            S.op('dve', lambda e: e.tensor_scalar(out=ptf8.t[:, 2:4], in0=ptf.t[:, :], scalar1=16.0, scalar2=None, op0=ALU.mult), reads=[ptf.b], writes=[ptf8.b])
            for hf in range(2):
                S.op('dve', lambda e, hf=hf: e.tensor_scalar(out=idxg.t[:, hf, :], in0=qoff.t[:, 0:32], scalar1=ptf8.t[:, hf:hf + 1], scalar2=None, op0=ALU.add),
                     reads=[qoff.b, ptf8.b], writes=[idxg.b])
                S.op('dve', lambda e, hf=hf: e.tensor_scalar(out=idxi.t[:, hf, :], in0=qoff.t[:, 0:16], scalar1=ptf8.t[:, 2 + hf:3 + hf], scalar2=None, op0=ALU.add),
                     reads=[qoff.b, ptf8.b], writes=[idxi.b])
            gi = 0
            for name, cache, scr, npc, idxt in (('ki', cki_d, scrKi, 16, None), ('k', ck_d, scrK, 32, None), ('v', cvv_d, scrV, 32, None)):
                scr_bufs[name] = Buf('scr_' + name)
                cview = cache.rearrange("(n q) d -> n (q d)", q=GW // cache.shape[1])
                for hf in range(2):
                    for pc in range(npc):
                        sg = stg[gi % 2]
                        gi += 1
                        ix = (idxi if npc == 16 else idxg)
                        col = ix.t[:, hf, pc:pc + 1]

                        def fn(e, sg=sg, cview=cview, col=col):
                            return e.indirect_dma_start(out=sg.t[:, :], out_offset=None, in_=cview,
                                                        in_offset=bass.IndirectOffsetOnAxis(ap=col, axis=0))
                        S.dma('pool', None, None, reads=[ix.b], writes=[sg.b], fn=fn)
                        dd = cache.shape[1]
                        tt = GW // dd
                        S.dma('pool', scr[pc * tt:(pc + 1) * tt, hf * 128:(hf + 1) * 128, :].rearrange("t p d -> p t d"),
                              sg.t[:, :].rearrange("p (t d) -> p t d", d=dd), reads=[sg.b], writes=[scr_bufs[name]])

        with ExitStack() as sA:
            wpool_bf = sb([128, 4, 128], BF16, stack=sA)
            h1 = sb([128, D], F32, stack=sA)
            S.dma('sp', h1.t[:, 0:512].rearrange("p (g d) -> p g d", g=4), wpool_d.rearrange("g c d -> c g d"), writes=[h1.b])
            S.dma('sp', h1.t[:, 512:1024], psc_d.partition_broadcast(128), writes=[h1.b])
            S.op('dve', lambda e: e.tensor_tensor(out=wpool_bf.t[:].rearrange("p g d -> p (g d)"), in0=h1.t[:, 0:512],
                                                  in1=h1.t[:, 512:1024], op=ALU.mult),
                 reads=[h1.b], writes=[wpool_bf.b])

            xt = [sb([128, D], F32, stack=sA) for _ in range(2)]
            zst = [sb([128, 1024], BF16, stack=sA) for _ in range(1)]
            st_ = [sb([128, 40], F32, stack=sA) for _ in range(2)]
            hb = sb([128, D], BF16, stack=sA)
            hT = [sb([128, 8, 128], BF16, stack=sA) for _ in range(2)]
            tq = sb([128, 512], F32, stack=sA)
            qraw = sb([128, 512], F32, stack=sA)
            kvraw = sb([128, 328], F32, stack=sA)
            qn = sb([128, 512], BF16, stack=sA)
            qis = sb([128, 512], F32, stack=sA)
            sgp = sb([128, 512], BF16, stack=sA)
            u_f = sb([128, 512], F32, stack=sA)
            u_bf = [sb([128, 512], BF16, stack=sA) for _ in range(2)]
            dT = sb([128, 512], BF16, stack=sA)
            kv = [sb([128, 400], F32, stack=sA) for _ in range(1)]
            kb = sb([128, 128], BF16, stack=sA)
            ki2 = sb([128, 128], F32, stack=sA)
            state4 = sb([128, 15, 128], F32, stack=sA)
            un4 = sb([128, 128], F32, stack=sA)
            d4 = sb([128, 128], F32, stack=sA)
            d4b = sb([128, 128], BF16, stack=sA)
            S.op('dve', lambda e: e.memset(state4.t[:], 0.0), writes=[state4.b])
            S.op('dve', lambda e: e.memset(un4.t[:], 0.0), writes=[un4.b])
            S.op('dve', lambda e: e.memset(d4.t[:], 0.0), writes=[d4.b])
            S.op('dve', lambda e: e.memset(d4b.t[:], 0.0), writes=[d4b.b])
            for g in range(4):
                S.dma('sp', state4.t[g * 32:g * 32 + NS, :, :], sp_d.rearrange("b (r c) -> b r c", r=15)[:, :, g * 128:(g + 1) * 128],
                      writes=[state4.b])
            S.dma('sp', ps_d[:, 0:14 * 512], sp_d[:, 512:15 * 512])

            class Defer:
                def __init__(self):
                    self.early, self.late = [], []

                def op(self, *a, early=False, **k):
                    (self.early if early else self.late).append(('op', a, k))

                def dma(self, *a, early=False, **k):
                    (self.early if early else self.late).append(('dma', a, k))

                def flush(self, which):
                    lst = self.early if which == 'early' else self.late
                    for kind, a, k in lst:
                        (S.op if kind == 'op' else S.dma)(*a, **k)
                    lst.clear()
            defer = {}

            def phaseA_tile(i, stage):
                samp = (i == NT)
                R = NS if samp else 128
                x = xt[i % 2]
                stt = st_[i % 2]
                kvt = kv[0]
                Asrc, Bsrc = (A_s.t[0:R, :], ada_sb.t[0:R, 0:D]) if samp else (A_p.t[:, :], B_p.t[:, :])
                Ab, Bb = (A_s.b, ada_sb.b) if samp else (A_p.b, B_p.b)
                if stage == 'load':
                    S.dma('sp', x.t[0:R, :], xs_d if samp else x_d[i * 128:(i + 1) * 128, :], writes=[x.b])
                    return
                hTt = hT[i % 2]
                if stage == 'pre':
                    phaseA_pre(i, samp, R, x, stt, Asrc, Bsrc, Ab, Bb, hTt)
                    return
                widths = [512, 512, 512, 512, 512, 328]
                if stage == 'proj':
                    for n in (5, 0, 1, 2, 3, 4):
                        c0 = n * 512
                        for k in range(8):
                            S.op('pe', lambda e, n=n, k=k, c0=c0: e.matmul(PB[n].t[0:R, 0:widths[n]], lhsT=hTt.t[:, k, 0:R],
                                                                           rhs=win_bf.t[:, k, c0:c0 + widths[n]], start=(k == 0), stop=(k == 7)),
                                 reads=[hTt.b, win_bf.bs[k]], writes=[PB[n].b])
                    return
                if stage == 'postA':
                    defer[i] = Defer()
                    phaseA_post(i, samp, R, x, stt, kvt, defer[i])
                    defer[i].flush('early')
                    return
                defer.pop(i).flush('late')

            def phaseA_pre(i, samp, R, x, stt, Asrc, Bsrc, Ab, Bb, hTt):
                S.op('act', lambda e: e.activation(out=h1.t[0:R, :], in_=x.t[0:R, :], func=AF.Square, accum_out=stt.t[0:R, 0:1]),
                     reads=[x.b], writes=[h1.b, stt.b])
                S.op('act', lambda e: e.activation(out=stt.t[0:R, 1:2], in_=stt.t[0:R, 0:1], func=AF.Ln, bias=eps_t.t[0:R, 0:1], scale=1.0 / D),
                     reads=[stt.b, eps_t.b], writes=[stt.b])
                S.op('act', lambda e: e.activation(out=stt.t[0:R, 2:3], in_=stt.t[0:R, 1:2], func=AF.Exp, scale=-0.5),
                     reads=[stt.b], writes=[stt.b])
                S.op('dve', lambda e: e.scalar_tensor_tensor(out=h1.t[0:R, :], in0=x.t[0:R, :], scalar=stt.t[0:R, 2:3], in1=Asrc,
                                                             op0=ALU.mult, op1=ALU.mult),
                     reads=[x.b, stt.b, Ab], writes=[h1.b])
                S.op('dve', lambda e: e.tensor_tensor(out=hb.t[0:R, :], in0=h1.t[0:R, :], in1=Bsrc, op=ALU.add),
                     reads=[h1.b, Bb], writes=[hb.b])
                tb = PB[6]
                for k in range(8):
                    S.op('pe', lambda e, k=k: e.transpose(out=bf_view(tb)[:, k * 128:k * 128 + R], in_=hb.t[0:R, k * 128:(k + 1) * 128],
                                                          identity=ident_b.t[0:R, 0:R]),
                         reads=[hb.b, ident_b.b], writes=[tb.b])
                S.op('act', lambda e: e.copy(out=hTt.t[:, :, 0:R], in_=bf_view(tb).rearrange("p (k r) -> p k r", k=8)[:, :, 0:R]),
                     reads=[tb.b], writes=[hTt.b])
                if DBG_DUMP and i == 0:
                    S.dma('sp', dbg['h1'], h1.t[:], reads=[h1.b])
                    S.dma('pool', dbg['hb'], hb.t[:], reads=[hb.b])
                    S.dma('pool', dbg['hT'], hTt.t[:].rearrange("p k r -> p (k r)"), reads=[hTt.b])
                    S.dma('pool', dbg['win'], win_bf.t[:, 0, :], reads=[win_bf.bs[0]])
                    S.dma('sp', dbg['stt'], stt.t[:], reads=[stt.b])

            def phaseA_post(i, samp, R, x, stt, kvt, E):
                P5r = PB[5]
                E.op('act', lambda e: e.copy(out=kvraw.t[0:R, :], in_=P5r.t[0:R, 0:328]), reads=[P5r.b], writes=[kvraw.b], early=True)
                P5 = kvraw
                E.op('act', lambda e: e.activation(out=tq.t[0:R, 0:128], in_=P5.t[0:R, 0:128], func=AF.Square), reads=[P5.b], writes=[tq.b])
                E.op('dve', lambda e: e.tensor_reduce(out=stt.t[0:R, 4:6], in_=tq.t[0:R, 0:128].rearrange("p (g d) -> p g d", g=2), axis=AX.X, op=ALU.add),
                     reads=[tq.b], writes=[stt.b])
                E.op('act', lambda e: e.activation(out=stt.t[0:R, 6:8], in_=stt.t[0:R, 4:6], func=AF.Ln, bias=eps_t.t[0:R, 0:1], scale=1.0 / 64),
                     reads=[stt.b, eps_t.b], writes=[stt.b])
                E.op('act', lambda e: e.activation(out=stt.t[0:R, 8:10], in_=stt.t[0:R, 6:8], func=AF.Exp, scale=-0.5),
                     reads=[stt.b], writes=[stt.b])
                E.op('dve', lambda e: e.tensor_tensor(out=kvt.t[0:R, 256:384].rearrange("p (g d) -> p g d", g=2),
                                                      in0=P5.t[0:R, 0:128].rearrange("p (g d) -> p g d", g=2),
                                                      in1=stt.t[0:R, 8:10].unsqueeze(2).to_broadcast([R, 2, 64]), op=ALU.mult),
                     reads=[P5.b, stt.b], writes=[kvt.b])
                kdst = kn_s.t[0:R, :] if samp else kvt.t[0:R, 0:128]
                kdb = kn_s.b if samp else kvt.b
                E.op('dve', lambda e: e.tensor_tensor(out=kdst, in0=kvt.t[0:R, 256:384], in1=kw_bc.t[0:R, :], op=ALU.mult),
                     reads=[kvt.b, kw_bc.b], writes=[kdb])
                vdst = v_s.t[0:R, :] if samp else kvt.t[0:R, 128:256]
                vdb = v_s.b if samp else kvt.b
                E.op('act', lambda e: e.copy(out=vdst, in_=P5.t[0:R, 128:256]), reads=[P5.b], writes=[vdb])
                if samp:
                    E.op('act', lambda e: e.copy(out=ki_s.t[0:R, :], in_=P5.t[0:R, 256:320]), reads=[P5.b], writes=[ki_s.b])
                    E.op('dve', lambda e: e.tensor_scalar(out=wis_s.t[0:R, :], in0=P5.t[0:R, 320:328], scalar1=512.0 ** -0.5, scalar2=None, op0=ALU.mult),
                         reads=[P5.b], writes=[wis_s.b], early=True)
                    E.dma('sp', ks_d, kn_s.t[:, :], reads=[kn_s.b])
                    E.dma('sp', vs_d, v_s.t[:, :], reads=[v_s.b])
                    E.dma('sp', kis_d, ki_s.t[:, :], reads=[ki_s.b])
                    wsrc, wb = wis_s.t[0:R, :], wis_s.b
                else:
                    E.op('act', lambda e: e.copy(out=ki2.t[:, 0:64], in_=P5.t[:, 256:320]), reads=[P5.b], writes=[ki2.b])
                    E.op('dve', lambda e: e.tensor_copy(out=ki2.t[:, 64:128], in_=P5.t[:, 256:320]), reads=[P5.b], writes=[ki2.b])
                    E.op('dve', lambda e: e.tensor_scalar(out=wis_all.t[:, i, :], in0=P5.t[:, 320:328], scalar1=512.0 ** -0.5, scalar2=None, op0=ALU.mult),
                         reads=[P5.b], writes=[wis_all.bs[i]], early=True)
                    E.dma('sp', kp_d[i * 128:(i + 1) * 128, :], kvt.t[:, 0:128], reads=[kvt.b])
                    E.dma('sp', vp_d[i * 128:(i + 1) * 128, :], kvt.t[:, 128:256], reads=[kvt.b])
                    E.dma('sp', kip_d[i * 128:(i + 1) * 128, :], ki2.t[:, 0:64], reads=[ki2.b])
                    wsrc, wb = wis_all.t[:, i, :], wis_all.bs[i]
                    E.op('dve', lambda e: e.tensor_copy(out=kb.t[:, :], in_=kvt.t[:, 0:128]), reads=[kvt.b], writes=[kb.b])
                    E.op('act', lambda e: e.copy(out=V_all.t[:, i, :, 0:64], in_=kvt.t[:, 128:256].rearrange("p (g d) -> p g d", g=2)),
                         reads=[kvt.b], writes=[V_all.bs[i]])
                    E.op('dve', lambda e: e.tensor_scalar(out=sgn_all.t[:, i, :], in0=wis_all.t[:, i, :], scalar1=0.0, scalar2=None, op0=ALU.is_ge),
                         reads=[wis_all.bs[i]], writes=[sgn_all.bs[i]])
                    E.op('dve', lambda e: e.tensor_scalar(out=sgn_all.t[:, i, :], in0=sgn_all.t[:, i, :], scalar1=2.0, scalar2=-1.0, op0=ALU.mult, op1=ALU.add),
                         reads=[sgn_all.bs[i]], writes=[sgn_all.bs[i]])
                E.op('dve', lambda e: e.scalar_tensor_tensor(out=stt.t[0:R, 16:24], in0=wsrc, scalar=-1.0, in1=wsrc, op0=ALU.mult, op1=ALU.max),
                     reads=[wb], writes=[stt.b], early=True)
                P0r = PB[0]
                E.op('act', lambda e: e.copy(out=qraw.t[0:R, :], in_=P0r.t[0:R, :]), reads=[P0r.b], writes=[qraw.b], early=True)
                P0 = qraw
                E.op('act', lambda e: e.activation(out=tq.t[0:R, 0:512], in_=P0.t[0:R, :], func=AF.Square), reads=[P0.b], writes=[tq.b])
                E.op('dve', lambda e: e.tensor_reduce(out=stt.t[0:R, 24:32], in_=tq.t[0:R, 0:512].rearrange("p (h d) -> p h d", h=8), axis=AX.X, op=ALU.add),
                     reads=[tq.b], writes=[stt.b])
                E.op('act', lambda e: e.activation(out=stt.t[0:R, 32:40], in_=stt.t[0:R, 24:32], func=AF.Ln, bias=eps_t.t[0:R, 0:1], scale=1.0 / 64),
                     reads=[stt.b, eps_t.b], writes=[stt.b])
                E.op('act', lambda e: e.activation(out=stt.t[0:R, 24:32], in_=stt.t[0:R, 32:40], func=AF.Exp, scale=-0.5),
                     reads=[stt.b], writes=[stt.b])
                E.op('dve', lambda e: e.tensor_tensor(out=tq.t[0:R, :].rearrange("p (h d) -> p h d", h=8),
                                                      in0=P0.t[0:R, :].rearrange("p (h d) -> p h d", h=8),
                                                      in1=stt.t[0:R, 24:32].unsqueeze(2).to_broadcast([R, 8, 64]), op=ALU.mult),
                     reads=[P0.b, stt.b], writes=[tq.b])
                qdst, qdb = (qn_s.t[0:R, :], qn_s.b) if samp else (qn.t[:, :], qn.b)
                E.op('dve', lambda e: e.tensor_tensor(out=qdst, in0=tq.t[0:R, :], in1=qw_bc.t[0:R, :], op=ALU.mult),
                     reads=[tq.b, qw_bc.b], writes=[qdb])
                P1 = PB[1]
                qidst, qidb = (qis_s.t[0:R, :], qis_s.b) if samp else (qis.t[:, :], qis.b)
                E.op('dve', lambda e: e.tensor_tensor(out=qidst.rearrange("p (h d) -> p h d", h=8),
                                                      in0=P1.t[0:R, :].rearrange("p (h d) -> p h d", h=8),
                                                      in1=stt.t[0:R, 16:24].unsqueeze(2).to_broadcast([R, 8, 64]), op=ALU.mult),
                     reads=[P1.b, stt.b], writes=[qidb], early=True)
                P2, P3, P4 = PB[2], PB[3], PB[4]
                zt = zst[0]
                gdst, gdb = (sga_s.t[0:R, :], sga_s.b) if samp else (zt.t[:, 0:512], zt.b)
                E.op('act', lambda e: e.activation(out=gdst, in_=P2.t[0:R, :], func=AF.Silu), reads=[P2.b], writes=[gdb], early=True)
                E.op('act', lambda e: e.activation(out=sgp.t[0:R, :], in_=P3.t[0:R, :], func=AF.Silu), reads=[P3.b], writes=[sgp.b], early=True)
                ub = u_bf[i % 2]
                ubp = u_bf[(i + 1) % 2]
                if samp or i == NT - 1:
                    E.op('act', lambda e: e.copy(out=u_f.t[0:R, :], in_=P4.t[0:R, :]), reads=[P4.b], writes=[u_f.b], early=True)
                if samp:
                    E.dma('sp', ps_d[:, 14 * 512:15 * 512], u_f.t[0:R, :], reads=[u_f.b])
                    for g in range(4):
                        E.dma('sp', un4.t[g * 32:g * 32 + NS, :], u_f.t[0:NS, g * 128:(g + 1) * 128], reads=[u_f.b], writes=[un4.b])
                    for g, w in enumerate((2, 4, 8, 16)):
                        ps_ = slice(g * 32, g * 32 + NS)
                        hv = state4.t[ps_, 15 - (w - 1):15, :].rearrange("p r c -> p c r")
                        E.op('dve', lambda e, hv=hv, ps_=ps_: e.tensor_reduce(out=d4.t[ps_, :], in_=hv, axis=AX.X, op=ALU.add),
                             reads=[state4.b], writes=[d4.b])
                        E.op('dve', lambda e, ps_=ps_, w=w: e.scalar_tensor_tensor(out=d4.t[ps_, :], in0=un4.t[ps_, :], scalar=(1.0 - w), in1=d4.t[ps_, :],
                                                                                  op0=ALU.mult, op1=ALU.add),
                             reads=[un4.b, d4.b], writes=[d4.b])
                        E.op('dve', lambda e, ps_=ps_, w=w: e.tensor_scalar(out=d4b.t[ps_, :], in0=d4.t[ps_, :], scalar1=1.0 / w, scalar2=None, op0=ALU.mult),
                             reads=[d4.b], writes=[d4b.b])
                    E.op('pe', lambda e: e.transpose(out=bf_view(PB[7])[:, 0:128], in_=d4b.t[:, :], identity=ident_b.t[:, :]),
                         reads=[d4b.b, ident_b.b], writes=[PB[7].b])
                    E.op('act', lambda e: e.copy(out=dT.t[:, 0:128], in_=bf_view(PB[7])[:, 0:128]), reads=[PB[7].b], writes=[dT.b])
                else:
                    E.op('dve', lambda e: e.tensor_copy(out=ub.t[:, :], in_=P4.t[:, :]), reads=[P4.b], writes=[ub.b], early=True)
                    if i == NT - 1:
                        E.dma('sp', pp_d, u_f.t[113:128, :], reads=[u_f.b])
                    for g in range(4):
                        kind = 0 if i == 0 else 1
                        E.op('pe', lambda e, g=g, kind=kind: e.matmul(PB[7].t[:, g * 128:(g + 1) * 128], lhsT=ub.t[:, g * 128:(g + 1) * 128],
                                                                      rhs=pmat.t[:, kind, g, :], start=True, stop=(i == 0)),
                             reads=[ub.b, pmat.b], writes=[PB[7].b])
                        if i > 0:
                            E.op('pe', lambda e, g=g: e.matmul(PB[7].t[:, g * 128:(g + 1) * 128], lhsT=ubp.t[:, g * 128:(g + 1) * 128],
                                                               rhs=pmat.t[:, 2, g, :], start=False, stop=True),
                                 reads=[ubp.b, pmat.b], writes=[PB[7].b])
                    E.op('act', lambda e: e.copy(out=dT.t[:, :], in_=PB[7].t[:, :]), reads=[PB[7].b], writes=[dT.b])
                for g in range(4):
                    dc = g * 32 if samp else g * 128
                    E.op('pe', lambda e, g=g, dc=dc: e.matmul(PB[7].t[0:R, g * 128:(g + 1) * 128], lhsT=dT.t[:, dc:dc + R],
                                                       rhs=wpool_bf.t[:, g, :], start=True, stop=True),
                         reads=[dT.b, wpool_bf.b], writes=[PB[7].b])
                zdst, zdb = (zp_s.t[0:R, :], zp_s.b) if samp else (zt.t[:, 512:1024], zt.b)
                E.op('dve', lambda e: e.tensor_tensor(out=zdst, in0=PB[7].t[0:R, :], in1=sgp.t[0:R, :], op=ALU.mult),
                     reads=[PB[7].b, sgp.b], writes=[zdb])
                if samp:
                    return
                E.dma('sp', zs_d[i * 128:(i + 1) * 128, :], zt.t[:, :], reads=[zt.b], writes=[zs_bufs[i]])
                tb = PB[6]
                for hh in range(4):
                    E.op('pe', lambda e, hh=hh: e.transpose(out=bf_view(tb)[:, hh * 128:(hh + 1) * 128],
                                                            in_=qn.t[:, hh * 128:(hh + 1) * 128],
                                                            identity=ident_b.t[:, :]),
                         reads=[qn.b, ident_b.b], writes=[tb.b])
                E.op('pe', lambda e: e.transpose(out=bf_view(tb)[:, 512:640], in_=kb.t[:, :], identity=ident_b.t[:, :]),
                     reads=[kb.b, ident_b.b], writes=[tb.b])
                E.op('act', lambda e: e.copy(out=qT_all.t[:, i, :], in_=bf_view(tb)[:, 0:512]), reads=[tb.b], writes=[qT_all.bs[i]])
                E.op('act', lambda e: e.copy(out=kT_all.t[:, i * 128:(i + 1) * 128], in_=bf_view(tb)[:, 512:640]), reads=[tb.b], writes=[kT_all.bs[i]])
                tb2 = PB[7]
                for hh in range(4):
                    E.op('pe', lambda e, hh=hh: e.transpose(out=tb2.t[:, hh * 128:(hh + 1) * 128],
                                                            in_=qis.t[:, hh * 128:(hh + 1) * 128],
                                                            identity=ident_f.t[:, :]),
                         reads=[qis.b, ident_f.b], writes=[tb2.b])
                E.op('dve', lambda e: e.tensor_copy(out=qiT_all.t[:, i, :], in_=tb2.t[:, :]), reads=[tb2.b], writes=[qiT_all.bs[i]])
                E.op('pe', lambda e: e.transpose(out=tb2.t[:, 0:128], in_=ki2.t[:, :], identity=ident_f.t[:, :]),
                     reads=[ki2.b, ident_f.b], writes=[tb2.b])
                E.op('act', lambda e: e.copy(out=kiT_all.t[:, i * 128:(i + 1) * 128], in_=tb2.t[:, 0:128]), reads=[tb2.b], writes=[kiT_all.bs[i]])

            if 'A' in phases:
                tl = list(DBG_TILES)
                phaseA_tile(tl[0], 'load')
                if len(tl) > 1:
                    phaseA_tile(tl[1], 'load')
                phaseA_tile(tl[0], 'pre')
                phaseA_tile(tl[0], 'proj')
                for n, i in enumerate(tl):
                    if n + 1 < len(tl):
                        phaseA_tile(tl[n + 1], 'pre')
                    if n + 2 < len(tl):
                        phaseA_tile(tl[n + 2], 'load')
                    phaseA_tile(i, 'postA')
                    if n + 1 < len(tl):
                        phaseA_tile(tl[n + 1], 'proj')
                    phaseA_tile(i, 'postB')

        sW.close()

        S.barrier()
        with ExitStack() as sB:
            wout_bf = sb([128, 8, D], BF16, 8, stack=sB)
            wstg = [sb([128, D], F32, stack=sB) for _ in range(2)]
            G_p = sb([128, D], F32, stack=sB)
            for hf in range(2):
                S.op('pe', lambda e, hf=hf: e.matmul(PB[6 + hf].t[:, :], lhsT=ones_f.t[32:33, 0:128],
                                                     rhs=ada_sb.t[32:33, 2 * D + hf * 512:2 * D + (hf + 1) * 512], start=True, stop=True),
                     reads=[ones_f.b, ada_sb.b], writes=[PB[6 + hf].b])
                S.op('act', lambda e, hf=hf: e.copy(out=G_p.t[:, hf * 512:(hf + 1) * 512], in_=PB[6 + hf].t[:, :]),
                     reads=[PB[6 + hf].b], writes=[G_p.b])
            for k in range(8):
                S.dma('sp', wstg[k % 2].t[:, :], wout_d[k * 128:(k + 1) * 128, :], writes=[wstg[k % 2].b])
                S.op('dve', lambda e, k=k: e.tensor_tensor(out=wout_bf.t[:, k, :], in0=wstg[k % 2].t[:, :], in1=G_p.t[:, :], op=ALU.mult),
                     reads=[wstg[k % 2].b, G_p.b], writes=[wout_bf.bs[k]])
            acc = [sb([128, T], F32, stack=sB) for _ in range(2)]
            Rr = [sb([128, 512], F32, stack=sB) for _ in range(2)]
            mb = [sb([128, T], BF16, stack=sB) for _ in range(2)]
            junkb = sb([128, T], BF16, stack=sB)
            PT = [sb([128, 512], BF16, stack=sB) for _ in range(6)]
            OT_sb = sb([65, 1024], F32, stack=sB)
            zt_ = sb([128, 512], BF16, stack=sB)
            zT = sb([128, 8, 128], BF16, stack=sB)
            xq = [sb([128, D], F32, stack=sB) for _ in range(2)]
            yo = [sb([128, D], F32, stack=sB) for _ in range(2)]
            zsl = [sb([128, 1024], BF16, stack=sB) for _ in range(2)]
            bs_ = [sb([128, 16 + 3 * NIT], F32, stack=sB) for _ in range(2)]
            junkb2 = None
            tmpd = sb([128, 128], F32, stack=sB)
            rden = sb([128, 8], F32, stack=sB)
            cnt_s = [0, 0]

            def phaseB_X(i, act_chain=False):
                L = 128 * (i + 1)
                nch = (L + 511) // 512
                ac = acc[i % 2]
                m_ = mb[i % 2]
                bs = bs_[i % 2]
                for j, ch in [(2 * jp + jj, ch) for jp in range(4) for ch in range(nch) for jj in range(2)]:
                    pr = slice((j % 2) * 64, (j % 2) * 64 + 64)
                    qc = slice((j // 2) * 128, (j // 2 + 1) * 128)
                    if True:
                        w = min(512, L - ch * 512)
                        cs = slice(ch * 512, ch * 512 + w)
                        bank = PB[cnt_s[0] % 2]
                        Rt = Rr[cnt_s[0] % 2]
                        cnt_s[0] += 1
                        kib = [kiT_all.bs[t] for t in range(ch * 4, min(ch * 4 + 4, i + 1))]
                        S.op('pe', lambda e, pr=pr, qc=qc, cs=cs, w=w, bank=bank: e.matmul(bank.t[:, 0:w], lhsT=qiT_all.t[pr, i, qc], rhs=kiT_all.t[pr, cs],
                                                                                          start=True, stop=True),
                             reads=[qiT_all.bs[i]] + kib, writes=[bank.b])
                        S.op('act', lambda e, w=w, bank=bank, Rt=Rt: e.activation(out=Rt.t[:, 0:w], in_=bank.t[:, 0:w], func=AF.Relu),
                             reads=[bank.b], writes=[Rt.b])
                        if j == 0:
                            S.op('dve', lambda e, w=w, cs=cs, Rt=Rt: e.tensor_scalar(out=ac.t[:, cs], in0=Rt.t[:, 0:w], scalar1=sgn_all.t[:, i, 0:1], scalar2=None,
                                                                                    op0=ALU.mult),
                                 reads=[Rt.b, sgn_all.bs[i]], writes=[ac.b])
                        else:
                            S.op('dve', lambda e, w=w, cs=cs, Rt=Rt, j=j: e.scalar_tensor_tensor(out=ac.t[:, cs], in0=Rt.t[:, 0:w], scalar=sgn_all.t[:, i, j:j + 1],
                                                                                                in1=ac.t[:, cs], op0=ALU.mult, op1=ALU.add),
                                 reads=[Rt.b, sgn_all.bs[i], ac.b], writes=[ac.b])
                dg = slice(i * 128, (i + 1) * 128)
                S.op('dve', lambda e: e.tensor_reduce(out=bs.t[:, 0:1], in_=ac.t[:, 0:L], axis=AX.X, op=ALU.max, apply_absolute_value=True), reads=[ac.b], writes=[bs.b])
                S.op('dve', lambda e: e.tensor_tensor(out=ac.t[:, dg], in0=ac.t[:, dg], in1=causal.t[:, :], op=ALU.add),
                     reads=[ac.b, causal.b], writes=[ac.b])
                S.op('dve', lambda e: e.tensor_scalar(out=bs.t[:, 1:2], in0=bs.t[:, 0:1], scalar1=-1.0, scalar2=None, op0=ALU.mult), reads=[bs.b], writes=[bs.b])
                S.op('dve', lambda e: e.tensor_scalar(out=bs.t[:, 4:5], in0=bs.t[:, 0:1], scalar1=2.0, scalar2=None, op0=ALU.mult), reads=[bs.b], writes=[bs.b])
                S.op('dve', lambda e: e.tensor_scalar(out=bs.t[:, 16:16 + NIT], in0=pow2.t[:, :], scalar1=bs.t[:, 4:5], scalar2=None, op0=ALU.mult),
                     reads=[bs.b, pow2.b], writes=[bs.b])
                def mask_op():
                    S.op('dve', lambda e: e.tensor_scalar(out=m_.t[:, 0:L], in0=ac.t[:, 0:L], scalar1=bs.t[:, 1:2], scalar2=NEG, op0=ALU.is_lt, op1=ALU.mult),
                         reads=[ac.b, bs.b], writes=[m_.b])
                if L <= TOPK:
                    mask_op()
                    return []
                if not act_chain:
                    S.op('dve', lambda e: e.tensor_scalar(out=bs.t[:, 16 + NIT:16 + 2 * NIT], in0=pow2.t[:, :], scalar1=bs.t[:, 4:5], scalar2=-0.5, op0=ALU.mult, op1=ALU.mult),
                         reads=[bs.b, pow2.b], writes=[bs.b])
                    S.op('dve', lambda e: e.tensor_tensor(out=bs.t[:, 1:2], in0=bs.t[:, 1:2], in1=bs.t[:, 16:17], op=ALU.add), reads=[bs.b], writes=[bs.b])
                    for it in range(NIT):
                        S.op('dve', lambda e: e.tensor_scalar(out=junkb.t[:, 0:L], in0=ac.t[:, 0:L], scalar1=bs.t[:, 1:2], scalar2=0.0, op0=ALU.is_ge, op1=ALU.add,
                                                              accum_out=bs.t[:, 6:7]),
                             reads=[ac.b, bs.b], writes=[junkb.b, bs.b])
                        S.op('dve', lambda e, it=it: e.tensor_scalar(out=bs.t[:, 7:8], in0=bs.t[:, 6:7], scalar1=TOPK - 0.5, scalar2=bs.t[:, 16 + it:17 + it],
                                                                     op0=ALU.is_gt, op1=ALU.mult),
                             reads=[bs.b], writes=[bs.b])
                        S.op('dve', lambda e, it=it: e.scalar_tensor_tensor(out=bs.t[:, 1:2], in0=bs.t[:, 1:2], scalar=bs.t[:, 16 + NIT + it:17 + NIT + it], in1=bs.t[:, 7:8],
                                                                            op0=ALU.add, op1=ALU.add),
                             reads=[bs.b], writes=[bs.b])
                    mask_op()
                    return []
                S.op('dve', lambda e: e.tensor_scalar(out=bs.t[:, 16 + NIT:16 + 2 * NIT], in0=pow2.t[:, :], scalar1=bs.t[:, 4:5], scalar2=-1.0, op0=ALU.mult, op1=ALU.mult),
                     reads=[bs.b, pow2.b], writes=[bs.b])
                S.op('dve', lambda e: e.tensor_scalar(out=bs.t[:, 16 + 2 * NIT:16 + 3 * NIT], in0=pow2.t[:, :], scalar1=bs.t[:, 4:5], scalar2=0.5, op0=ALU.mult, op1=ALU.mult),
                     reads=[bs.b, pow2.b], writes=[bs.b])
                steps = []
                for it in range(NIT):
                    def step(it=it):
                        nW = bs.t[:, 16 + NIT + it:17 + NIT + it]
                        hW = bs.t[:, 16 + 2 * NIT + it:17 + 2 * NIT + it]
                        S.op('act', lambda e: e.activation(out=bs.t[:, 5:6], in_=bs.t[:, 1:2], func=AF.Identity, scale=-1.0, bias=nW), reads=[bs.b], writes=[bs.b])
                        S.op('act', lambda e: e.activation(out=bs.t[:, 8:9], in_=bs.t[:, 1:2], func=AF.Identity, scale=1.0, bias=hW), reads=[bs.b], writes=[bs.b])
                        S.op('act', lambda e: e.activation(out=junkb2.t[:, 0:L], in_=ac.t[:, 0:L], func=AF.Sign, bias=bs.t[:, 5:6], scale=1.0, accum_out=bs.t[:, 6:7]),
                             reads=[ac.b, bs.b], writes=[junkb2.b, bs.b])
                        S.op('act', lambda e: e.activation(out=bs.t[:, 7:8], in_=bs.t[:, 6:7], func=AF.Sign, bias=float(L - 2 * (TOPK - 0.5)), scale=1.0),
                             reads=[bs.b], writes=[bs.b])
                        S.op('act', lambda e: e.activation(out=bs.t[:, 1:2], in_=bs.t[:, 7:8], func=AF.Identity, scale=hW, bias=bs.t[:, 8:9]), reads=[bs.b], writes=[bs.b])
                    steps.append(step)
                steps.append(mask_op)
                return steps

            def phaseB_Y(i, inter=()):
                inter = list(inter)
                nblk = 2 * (i + 1)
                done_blk = [0]
                m_ = mb[i % 2]
                x = xq[i % 2]
                zs = zsl[i % 2]
                S.dma('sp', x.t[:, :], x_d[i * 128:(i + 1) * 128, :], writes=[x.b])
                S.dma('sp', zs.t[:, :], zs_d[i * 128:(i + 1) * 128, :], reads=[zs_bufs[i]], writes=[zs.b])
                stb = [PB[2], PB[3], PB[0], PB[1]]
                for jb in range(i + 1):
                    lb = slice(jb * 128, (jb + 1) * 128)
                    banks = [stb[(2 * cnt_s[1]) % 4], stb[(2 * cnt_s[1] + 1) % 4]]
                    Pts = [PT[(2 * cnt_s[1]) % 6], PT[(2 * cnt_s[1] + 1) % 6]]
                    cnt_s[1] += 1
                    for g in range(2):
                        gp = slice(g * 64, (g + 1) * 64)
                        S.op('pe', lambda e, gp=gp, lb=lb, bank=banks[g]: e.matmul(bank.t[:, :], lhsT=kT_all.t[gp, lb], rhs=qT_all.t[gp, i, :], start=True, stop=False),
                             reads=[kT_all.bs[jb], qT_all.bs[i]], writes=[banks[g].b])
                    for g in range(2):
                        S.op('pe', lambda e, lb=lb, bank=banks[g]: e.matmul(bank.t[:, :], lhsT=m_.t[:, lb], rhs=ident4_b.t[:, :], start=False, stop=True),
                             reads=[m_.b, ident4_b.b], writes=[banks[g].b])
                    for g in range(2):
                        S.op('act', lambda e, bank=banks[g], Pt=Pts[g]: e.activation(out=Pt.t[:, :], in_=bank.t[:, :], func=AF.Exp), reads=[banks[g].b], writes=[Pts[g].b])
                    for g in range(2):
                        S.op('pe', lambda e, g=g, jb=jb, Pt=Pts[g]: e.matmul(PB[4 + g].t[:, :], lhsT=V_all.t[:, jb, g, :], rhs=Pt.t[:, :], start=(jb == 0), stop=(jb == i)),
                             reads=[V_all.bs[jb], Pts[g].b], writes=[PB[4 + g].b])
                    done_blk[0] += 2
                    while inter and (len(inter) > (nblk - done_blk[0]) * (NIT + 1) // nblk):
                        inter.pop(0)()
                while inter:
                    inter.pop(0)()
                for g in range(2):
                    S.op('act', lambda e, g=g: e.copy(out=OT_sb.t[:, g * 512:(g + 1) * 512], in_=PB[4 + g].t[0:65, :]), reads=[PB[4 + g].b], writes=[OT_sb.b])
                    tb = PB[6]
                    for hp in range(4):
                        S.op('pe', lambda e, g=g, hp=hp: e.transpose(out=tb.t[:, hp * 65:(hp + 1) * 65], in_=OT_sb.t[0:65, g * 512 + hp * 128:g * 512 + (hp + 1) * 128],
                                                                     identity=ident_f.t[0:65, 0:65]),
                             reads=[OT_sb.b, ident_f.b], writes=[tb.b])
                    S.op('dve', lambda e, g=g: e.reciprocal(out=rden.t[:, g * 4:(g + 1) * 4], in_=tb.t[:, 0:260].rearrange("p (h c) -> p h c", h=4)[:, :, 64]),
                         reads=[tb.b], writes=[rden.b])
                    for hp in range(4):
                        h = g * 4 + hp
                        S.op('dve', lambda e, hp=hp, h=h: e.scalar_tensor_tensor(out=zt_.t[:, h * 64:(h + 1) * 64], in0=tb.t[:, hp * 65:hp * 65 + 64],
                                                                                 scalar=rden.t[:, h:h + 1], in1=zs.t[:, h * 64:(h + 1) * 64],
                                                                                 op0=ALU.mult, op1=ALU.mult),
                             reads=[tb.b, rden.b, zs.b], writes=[zt_.b])
                tb = PB[6]
                for c in range(8):
                    src = zt_.t[:, c * 128:(c + 1) * 128] if c < 4 else zs.t[:, 512 + (c - 4) * 128:512 + (c - 3) * 128]
                    S.op('pe', lambda e, c=c, src=src: e.transpose(out=bf_view(tb)[:, c * 128:(c + 1) * 128], in_=src, identity=ident_b.t[:, :]),
                         reads=[zt_.b, zs.b, ident_b.b], writes=[tb.b])
                S.op('act', lambda e: e.copy(out=zT.t[:].rearrange("p c q -> p (c q)"), in_=bf_view(tb)), reads=[tb.b], writes=[zT.b])
                yt = yo[i % 2]
                for n in range(2):
                    bank = PB[7] if n == 0 else PB[6]
                    for c in range(8):
                        S.op('pe', lambda e, n=n, c=c, bank=bank: e.matmul(bank.t[:, :], lhsT=zT.t[:, c, :], rhs=wout_bf.t[:, c, n * 512:(n + 1) * 512],
                                                                           start=(c == 0), stop=(c == 7)),
                             reads=[zT.b, wout_bf.bs[c]], writes=[bank.b])
                    ns = slice(n * 512, (n + 1) * 512)
                    S.op('dve', lambda e, ns=ns, bank=bank: e.tensor_tensor(out=yt.t[:, ns], in0=bank.t[:, :], in1=x.t[:, ns], op=ALU.add),
                         reads=[bank.b, x.b], writes=[yt.b])
                S.dma('sp', yp_d[i * 128:(i + 1) * 128, :], yt.t[:, :], reads=[yt.b])

            if 'B' in phases:
                tl = [i for i in DBG_TILES if i < NT]
                use_act = lambda t: False
                for n, i in enumerate(tl):
                    if n == 0:
                        phaseB_X(i)
                    inter = []
                    if n + 1 < len(tl):
                        inter = phaseB_X(tl[n + 1], act_chain=use_act(tl[n + 1]))
                    phaseB_Y(i, inter)

        S.barrier()
        if 'S' in phases:
          with ExitStack() as sS:
            NV = T + NS
            wout_bf = sb([128, 8, D], BF16, 8, stack=sS)
            for k in range(8):
                S.dma('pool', wout_bf.t[:, k, :], wout_d[k * 128:(k + 1) * 128, :], writes=[wout_bf.bs[k]])
            ki_st = [sb([128, NPG, 64], F32, stack=sS) for _ in range(2)]
            kiT_c = [sb([64, 512], F32R, stack=sS) for _ in range(3)]
            Zq = sb([64, NS, 128], F32R, stack=sS)
            zero_f = sb([128, 512], F32, stack=sS)
            qiT_sall = sb([64, 128], F32R, stack=sS)
            kiT_new = sb([64, NS], F32R, stack=sS)
            R_sb = sb([128, NV], F32, stack=sS)
            Wg_a = sb([NS, 128], F32, stack=sS)
            Wg = sb([128, NS], F32, stack=sS)
            sgn_s = sb([NS, 8], F32, stack=sS)
            score = sb([NS, NV], F32, stack=sS)
            junks = sb([NS, NV], BF16, stack=sS)
            mbs = sb([NS, NV], BF16, stack=sS)
            bss = sb([NS, 16 + NIT], F32, stack=sS)
            tmps = sb([NS, NS], F32, stack=sS)
            S.op('pool', lambda e: e.memset(zero_f.t[:], 0.0), writes=[zero_f.b])
            for j in range(8):
                S.op('pe', lambda e, j=j: e.transpose(out=PB[6].t[0:64, j * 16:(j + 1) * 16], in_=qis_s.t[0:NS, j * 64:(j + 1) * 64], identity=ident_f.t[0:NS, 0:NS]),
                     reads=[qis_s.b, ident_f.b], writes=[PB[6].b])
            S.op('dve', lambda e: e.tensor_copy(out=qiT_sall.t[:, :].rearrange("d (b j) -> d j b", j=8),
                                                in_=PB[6].t[0:64, 0:128].rearrange("d (j b) -> d j b", j=8)),
                 reads=[PB[6].b], writes=[qiT_sall.b])
            for q4 in range(4):
                S.op('dve', lambda e, q4=q4: e.tensor_scalar(out=Zq.t[:, q4 * 4:(q4 + 1) * 4, :].rearrange("d b c -> d (b c)"), in0=zero_f.t[0:64, :], scalar1=0.0, scalar2=None,
                                                             op0=ALU.mult),
                     reads=[zero_f.b], writes=[Zq.b])
            for b in range(NS):
                S.op('dve', lambda e, b=b: e.tensor_copy(out=Zq.t[:, b, b * 8:(b + 1) * 8], in_=qiT_sall.t[:, b * 8:(b + 1) * 8]),
                     reads=[qiT_sall.b], writes=[Zq.b])
            S.op('pe', lambda e: e.transpose(out=PB[7].t[0:64, 0:NS], in_=ki_s.t[0:NS, :], identity=ident_f.t[0:NS, 0:NS]),
                 reads=[ki_s.b, ident_f.b], writes=[PB[7].b])
            S.op('act', lambda e: e.copy(out=kiT_new.t[:, :], in_=PB[7].t[0:64, 0:NS]), reads=[PB[7].b], writes=[kiT_new.b])
            S.op('dve', lambda e: e.tensor_scalar(out=sgn_s.t[:, :], in0=wis_s.t[:, :], scalar1=0.0, scalar2=None, op0=ALU.is_ge), reads=[wis_s.b], writes=[sgn_s.b])
            S.op('dve', lambda e: e.tensor_scalar(out=sgn_s.t[:, :], in0=sgn_s.t[:, :], scalar1=2.0, scalar2=-1.0, op0=ALU.mult, op1=ALU.add),
                 reads=[sgn_s.b], writes=[sgn_s.b])
            S.op('dve', lambda e: e.tensor_tensor(out=Wg_a.t[:, :].rearrange("p (b j) -> p b j", j=8), in0=bsel.t[:, :].rearrange("p (b j) -> p b j", j=8),
                                                  in1=sgn_s.t[:, :].unsqueeze(1).to_broadcast([NS, NS, 8]), op=ALU.mult),
                 reads=[bsel.b, sgn_s.b], writes=[Wg_a.b])
            S.op('pe', lambda e: e.transpose(out=PB[7].t[:, 32:32 + NS], in_=Wg_a.t[0:NS, :], identity=ident_f.t[0:NS, 0:NS]),
                 reads=[Wg_a.b, ident_f.b], writes=[PB[7].b])
            S.op('act', lambda e: e.copy(out=Wg.t[:, :], in_=PB[7].t[:, 32:32 + NS]), reads=[PB[7].b], writes=[Wg.b])
            cc = 0
            for b in range(NS):
                kst = ki_st[b % 2]
                S.dma('sp', kst.t[:, :, :], scrKi[:, b * NPG:(b + 1) * NPG, :], reads=[scr_bufs['ki']], writes=[kst.b])
                for ch in range(4):
                    tb = PB[6 + cc % 2]
                    kc = kiT_c[cc % 3]
                    cc += 1
                    for p4 in range(4):
                        S.op('pe', lambda e, ch=ch, p4=p4, tb=tb, kst=kst: e.transpose(out=tb.t[0:64, p4 * 128:(p4 + 1) * 128], in_=kst.t[:, ch * 4 + p4, :],
                                                                                      identity=ident_f.t[:, :]),
                             reads=[kst.b, ident_f.b], writes=[tb.b])
                    S.op('act' if cc % 2 else 'dve', (lambda e, tb=tb, kc=kc: e.copy(out=kc.t[:, :], in_=tb.t[0:64, :])) if cc % 2 else
                         (lambda e, tb=tb, kc=kc: e.tensor_copy(out=kc.t[:, :], in_=tb.t[0:64, :])),
                         reads=[tb.b], writes=[kc.b])
                    S.op('pe', lambda e, b=b, ch=ch, kc=kc: e.matmul(PB[ch].t[:, :], lhsT=Zq.t[:, b, :], rhs=kc.t[:, :], start=(b == 0), stop=(b == NS - 1)),
                         reads=[Zq.b, kc.b], writes=[PB[ch].b])
            S.op('pe', lambda e: e.matmul(PB[4].t[:, 0:NS], lhsT=qiT_sall.t[:, :], rhs=kiT_new.t[:, :], start=True, stop=True),
                 reads=[qiT_sall.b, kiT_new.b], writes=[PB[4].b])
            for ch in range(4):
                S.op('act', lambda e, ch=ch: e.activation(out=R_sb.t[:, ch * 512:(ch + 1) * 512], in_=PB[ch].t[:, :], func=AF.Relu), reads=[PB[ch].b], writes=[R_sb.b])
            S.op('act', lambda e: e.activation(out=R_sb.t[:, T:NV], in_=PB[4].t[:, 0:NS], func=AF.Relu), reads=[PB[4].b], writes=[R_sb.b])
            for ch in range(5):
                w = 512 if ch < 4 else NS
                S.op('pe', lambda e, ch=ch, w=w: e.matmul(PB[5].t[0:NS, 0:w], lhsT=Wg.t[:, :], rhs=R_sb.t[:, ch * 512:ch * 512 + w], start=True, stop=True),
                     reads=[Wg.b, R_sb.b], writes=[PB[5].b])
                S.op('act', lambda e, ch=ch, w=w: e.copy(out=score.t[:, ch * 512:ch * 512 + w], in_=PB[5].t[0:NS, 0:w]), reads=[PB[5].b], writes=[score.b])
            vg = slice(T, NV)
            S.op('dve', lambda e: e.scalar_tensor_tensor(out=tmps.t[:, :], in0=offd.t[:, :], scalar=-1.0, in1=score.t[:, vg], op0=ALU.mult, op1=ALU.add),
                 reads=[offd.b, score.b], writes=[tmps.b])
            S.op('dve', lambda e: e.tensor_tensor(out=score.t[:, vg], in0=score.t[:, vg], in1=offd.t[:, :], op=ALU.add), reads=[score.b, offd.b], writes=[score.b])
            S.op('dve', lambda e: e.tensor_reduce(out=bss.t[:, 0:1], in_=score.t[:, :], axis=AX.X, op=ALU.max), reads=[score.b], writes=[bss.b])
            S.op('dve', lambda e: e.tensor_reduce(out=bss.t[:, 1:2], in_=tmps.t[:, :], axis=AX.X, op=ALU.min), reads=[tmps.b], writes=[bss.b])
            S.op('dve', lambda e: e.tensor_reduce(out=bss.t[:, 3:4], in_=score.t[:, 0:T], axis=AX.X, op=ALU.min), reads=[score.b], writes=[bss.b])
            S.op('dve', lambda e: e.tensor_tensor(out=bss.t[:, 1:2], in0=bss.t[:, 1:2], in1=bss.t[:, 3:4], op=ALU.min), reads=[bss.b], writes=[bss.b])
            S.op('dve', lambda e: e.tensor_tensor(out=bss.t[:, 4:5], in0=bss.t[:, 0:1], in1=bss.t[:, 1:2], op=ALU.subtract), reads=[bss.b], writes=[bss.b])
            S.op('dve', lambda e: e.tensor_scalar(out=bss.t[:, 16:16 + NIT], in0=pow2.t[0:NS, :], scalar1=bss.t[:, 4:5], scalar2=None, op0=ALU.mult),
                 reads=[bss.b, pow2.b], writes=[bss.b])
            for it in range(NIT):
                S.op('dve', lambda e, it=it: e.tensor_tensor(out=bss.t[:, 5:6], in0=bss.t[:, 1:2], in1=bss.t[:, 16 + it:17 + it], op=ALU.add), reads=[bss.b], writes=[bss.b])
                S.op('dve', lambda e: e.tensor_scalar(out=junks.t[:, :], in0=score.t[:, :], scalar1=bss.t[:, 5:6], scalar2=0.0, op0=ALU.is_ge, op1=ALU.add,
                                                      accum_out=bss.t[:, 6:7]),
                     reads=[score.b, bss.b], writes=[junks.b, bss.b])
                S.op('dve', lambda e, it=it: e.tensor_scalar(out=bss.t[:, 7:8], in0=bss.t[:, 6:7], scalar1=TOPK - 0.5, scalar2=bss.t[:, 16 + it:17 + it],
                                                             op0=ALU.is_gt, op1=ALU.mult),
                     reads=[bss.b], writes=[bss.b])
                S.op('dve', lambda e: e.tensor_tensor(out=bss.t[:, 1:2], in0=bss.t[:, 1:2], in1=bss.t[:, 7:8], op=ALU.add), reads=[bss.b], writes=[bss.b])
            S.op('dve', lambda e: e.tensor_scalar(out=mbs.t[:, :], in0=score.t[:, :], scalar1=bss.t[:, 1:2], scalar2=NEG, op0=ALU.is_lt, op1=ALU.mult),
                 reads=[score.b, bss.b], writes=[mbs.b])
            Qblk = sb([128, NS, 8], BF16, stack=sS)
            kT_new = sb([128, NS], BF16, stack=sS)
            v_new = sb([NS, 128], BF16, stack=sS)
            esel = sb([NS, 128], BF16, stack=sS)
            S.op('pool', lambda e: e.memset(Qblk.t[:], 0.0), writes=[Qblk.b])
            for hp in range(4):
                S.op('pe', lambda e, hp=hp: e.transpose(out=PB[6].t[:, hp * NS:(hp + 1) * NS], in_=qn_s.t[0:NS, hp * 128:(hp + 1) * 128], identity=ident_f.t[0:NS, 0:NS]),
                     reads=[qn_s.b, ident_f.b], writes=[PB[6].b])
            for hp in range(4):
                S.op('dve', lambda e, hp=hp: e.tensor_copy(out=Qblk.t[0:64, :, hp], in_=PB[6].t[0:64, hp * NS:(hp + 1) * NS]), reads=[PB[6].b], writes=[Qblk.b])
                S.op('dve', lambda e, hp=hp: e.tensor_copy(out=Qblk.t[64:128, :, 4 + hp], in_=PB[6].t[64:128, hp * NS:(hp + 1) * NS]), reads=[PB[6].b], writes=[Qblk.b])
            S.op('pe', lambda e: e.transpose(out=PB[7].t[:, 0:NS], in_=kn_s.t[0:NS, :], identity=ident_f.t[0:NS, 0:NS]), reads=[kn_s.b, ident_f.b], writes=[PB[7].b])
            S.op('act', lambda e: e.copy(out=kT_new.t[:, :], in_=PB[7].t[:, 0:NS]), reads=[PB[7].b], writes=[kT_new.b])
            S.op('dve', lambda e: e.tensor_copy(out=v_new.t[:, :], in_=v_s.t[:, :]), reads=[v_s.b], writes=[v_new.b])
            S.op('dve', lambda e: e.tensor_copy(out=esel.t[:, :], in_=bsel.t[:, :]), reads=[bsel.b], writes=[esel.b])
            k_st = [sb([128, NPG, 128], BF16, stack=sS) for _ in range(2)]
            v_st = [sb([128, NPG, 128], BF16, stack=sS) for _ in range(2)]
            kT_sb = [sb([128, T], BF16, stack=sS) for _ in range(2)]
            PTs = [sb([128, 136], BF16, stack=sS) for _ in range(2)]
            dn_all = sb([128, 128], F32, stack=sS)

            def p2_load(b):
                kst, vst = k_st[b % 2], v_st[b % 2]
                S.dma('pool', kst.t[:, :, :], scrK[:, b * NPG:(b + 1) * NPG, :], reads=[scr_bufs['k']], writes=[kst.b])
                S.dma('pool', vst.t[:, :, :], scrV[:, b * NPG:(b + 1) * NPG, :], reads=[scr_bufs['v']], writes=[vst.b])

            def p2_T(b):
                kst, kT = k_st[b % 2], kT_sb[b % 2]
                for hf in range(2):
                    tb = PB[hf]
                    for p8 in range(8):
                        S.op('pe', lambda e, hf=hf, p8=p8, tb=tb, kst=kst: e.transpose(out=bf_view(tb)[:, p8 * 128:(p8 + 1) * 128], in_=kst.t[:, hf * 8 + p8, :],
                                                                                      identity=ident_b.t[:, :]),
                             reads=[kst.b, ident_b.b], writes=[tb.b])
                    S.op('act' if hf else 'dve', (lambda e, tb=tb, kT=kT, hf=hf: e.copy(out=kT.t[:, hf * 1024:(hf + 1) * 1024], in_=bf_view(tb))) if hf else
                         (lambda e, tb=tb, kT=kT, hf=hf: e.tensor_copy(out=kT.t[:, hf * 1024:(hf + 1) * 1024], in_=bf_view(tb))),
                         reads=[tb.b], writes=[kT.b])

            def p2_L(b):
                kT = kT_sb[b % 2]
                LT = PB[2 + b % 2]
                Pt = PTs[b % 2]
                es = esel.t[:, b * 8:(b + 1) * 8]
                for jj in range(NPG):
                    S.op('pe', lambda e, jj=jj, LT=LT, kT=kT, b=b: e.matmul(LT.t[:, jj * 8:(jj + 1) * 8], lhsT=kT.t[:, jj * 128:(jj + 1) * 128], rhs=Qblk.t[:, b, :],
                                                                           start=True, stop=False),
                         reads=[kT.b, Qblk.b], writes=[LT.b])
                    S.op('pe', lambda e, jj=jj, LT=LT, es=es: e.matmul(LT.t[:, jj * 8:(jj + 1) * 8], lhsT=mbs.t[:, jj * 128:(jj + 1) * 128], rhs=es, start=False, stop=True),
                         reads=[mbs.b, esel.b], writes=[LT.b])
                S.op('pe', lambda e, LT=LT, b=b: e.matmul(LT.t[0:NS, 128:136], lhsT=kT_new.t[:, :], rhs=Qblk.t[:, b, :], start=True, stop=False),
                     reads=[kT_new.b, Qblk.b], writes=[LT.b])
                S.op('pe', lambda e, LT=LT, es=es: e.matmul(LT.t[0:NS, 128:136], lhsT=mbs.t[:, vg], rhs=es, start=False, stop=True),
                     reads=[mbs.b, esel.b], writes=[LT.b])
                S.op('act', lambda e, LT=LT, Pt=Pt: e.activation(out=Pt.t[:, 0:128], in_=LT.t[:, 0:128], func=AF.Exp), reads=[LT.b], writes=[Pt.b])
                S.op('act', lambda e, LT=LT, Pt=Pt: e.activation(out=Pt.t[0:NS, 128:136], in_=LT.t[0:NS, 128:136], func=AF.Exp), reads=[LT.b], writes=[Pt.b])

            def p2_PV(b):
                vst = v_st[b % 2]
                Pt = PTs[b % 2]
                oc = slice(b * 8, (b + 1) * 8)
                for jj in range(NPG):
                    S.op('pe', lambda e, jj=jj, oc=oc, Pt=Pt, vst=vst: e.matmul(PB[4].t[:, oc], lhsT=vst.t[:, jj, :], rhs=Pt.t[:, jj * 8:(jj + 1) * 8], start=(jj == 0), stop=False),
                         reads=[vst.b, Pt.b], writes=[PB[4].b])
                S.op('pe', lambda e, oc=oc, Pt=Pt: e.matmul(PB[4].t[:, oc], lhsT=v_new.t[:, :], rhs=Pt.t[0:NS, 128:136], start=False, stop=True),
                     reads=[v_new.b, Pt.b], writes=[PB[4].b])
                S.op('pe', lambda e, Pt=Pt: e.matmul(PB[5].t[:, 0:128], lhsT=ones_b.t[:, :], rhs=Pt.t[:, 0:128], start=True, stop=True),
                     reads=[ones_b.b, Pt.b], writes=[PB[5].b])
                S.op('pe', lambda e, Pt=Pt: e.matmul(PB[5].t[:, 128:136], lhsT=ones_b.t[0:NS, :], rhs=Pt.t[0:NS, 128:136], start=True, stop=True),
                     reads=[ones_b.b, Pt.b], writes=[PB[5].b])
                S.op('dve', lambda e, oc=oc: e.tensor_reduce(out=dn_all.t[:, oc], in_=PB[5].t[:, 0:136].rearrange("p (j h) -> p h j", h=8), axis=AX.X, op=ALU.add),
                     reads=[PB[5].b], writes=[dn_all.b])

            p2_load(0)
            p2_T(0)
            for b in range(NS):
                if b + 1 < NS:
                    p2_load(b + 1)
                p2_L(b)
                if b + 1 < NS:
                    p2_T(b + 1)
                p2_PV(b)
            rdn = sb([128, 128], F32, stack=sS)
            z_s = sb([NS, D], BF16, stack=sS)
            zT_s = sb([128, 8, NS], BF16, stack=sS)
            class _V:
                pass
            xs_sb = _V(); xs_sb.t = R_sb.t[0:NS, 0:D]; xs_sb.b = R_sb.b
            ys_sb = _V(); ys_sb.t = R_sb.t[0:NS, D:2 * D]; ys_sb.b = R_sb.b
            S.dma('sp', xs_sb.t[:, :], xs_d, writes=[xs_sb.b])
            S.op('dve', lambda e: e.reciprocal(out=rdn.t[:, :], in_=dn_all.t[:, :]), reads=[dn_all.b], writes=[rdn.b])
            aTg = [sb([64, 8, NS], F32, stack=sS) for _ in range(2)]
            for g in range(2):
                gp = slice(g * 64, (g + 1) * 64)
                S.op('dve', lambda e, g=g, gp=gp: e.tensor_tensor(out=aTg[g].t[:, :, :], in0=PB[4].t[gp, 0:128].rearrange("p (b h) -> p h b", h=8),
                                                                  in1=rdn.t[gp, :].rearrange("p (b h) -> p h b", h=8), op=ALU.mult),
                     reads=[PB[4].b, rdn.b], writes=[aTg[g].b])
            for h in range(8):
                g = h // 4
                S.op('pe', lambda e, h=h, g=g: e.transpose(out=PB[6].t[0:NS, h * 64:(h + 1) * 64], in_=aTg[g].t[:, h, :], identity=ident_f.t[0:64, 0:64]),
                     reads=[aTg[g].b, ident_f.b], writes=[PB[6].b])
            S.op('dve', lambda e: e.tensor_tensor(out=z_s.t[:, 0:512], in0=PB[6].t[0:NS, :], in1=sga_s.t[:, :], op=ALU.mult), reads=[PB[6].b, sga_s.b], writes=[z_s.b])
            S.op('dve', lambda e: e.tensor_copy(out=z_s.t[:, 512:1024], in_=zp_s.t[:, :]), reads=[zp_s.b], writes=[z_s.b])
            for c in range(8):
                S.op('pe', lambda e, c=c: e.transpose(out=bf_view(PB[7])[:, c * NS:(c + 1) * NS], in_=z_s.t[0:NS, c * 128:(c + 1) * 128], identity=ident_b.t[0:NS, 0:NS]),
                     reads=[z_s.b, ident_b.b], writes=[PB[7].b])
            S.op('act', lambda e: e.copy(out=zT_s.t[:].rearrange("p c b -> p (c b)"), in_=bf_view(PB[7])[:, 0:8 * NS]), reads=[PB[7].b], writes=[zT_s.b])
            for n in range(2):
                for c in range(8):
                    S.op('pe', lambda e, n=n, c=c: e.matmul(PB[n].t[0:NS, :], lhsT=zT_s.t[:, c, :], rhs=wout_bf.t[:, c, n * 512:(n + 1) * 512], start=(c == 0), stop=(c == 7)),
                         reads=[zT_s.b, wout_bf.bs[c]], writes=[PB[n].b])
                ns = slice(n * 512, (n + 1) * 512)
                S.op('dve', lambda e, n=n, ns=ns: e.tensor_tensor(out=ys_sb.t[:, ns], in0=PB[n].t[0:NS, :], in1=ada_sb.t[0:NS, 2 * D + n * 512:2 * D + (n + 1) * 512], op=ALU.mult),
                     reads=[PB[n].b, ada_sb.b], writes=[ys_sb.b])
                S.op('dve', lambda e, ns=ns: e.tensor_tensor(out=ys_sb.t[:, ns], in0=ys_sb.t[:, ns], in1=xs_sb.t[:, ns], op=ALU.add), reads=[ys_sb.b, xs_sb.b], writes=[ys_sb.b])
            S.dma('sp', ys_d, ys_sb.t[:, :], reads=[ys_sb.b])

        S.build()
        print("total ops", S.nops)
    return nc


_CACHE = {}


PHASES = '0ASB'
DBG_TILES = list(range(NT + 1))


def _get_program():
    if 'nc' not in _CACHE:
        _CACHE['nc'] = build_program(PHASES)
    return _CACHE['nc']


def kernel(x_prompt, x_sample, cache_k, cache_v, cache_kidx, state_pool, page_table, c_prompt, c_sample,
           norm_w, w_ada, b_ada, w_in, q_norm_w, k_norm_w, w_pool, pool_scale, w_out):
    f32 = np.float32
    nc = _get_program()
    consts = _consts()
    perm = _win_perm()
    win_p = np.ascontiguousarray(np.asarray(w_in, f32)[0][:, perm])
    ck = np.ascontiguousarray(np.asarray(cache_k, f32)[0].reshape(2560 * 128, 128))
    cv = np.ascontiguousarray(np.asarray(cache_v, f32)[0].reshape(2560 * 128, 128))
    cki = np.ascontiguousarray(np.asarray(cache_kidx, f32)[0].reshape(2560 * 128, 64))
    shared = dict(norm_w=np.asarray(norm_w, f32).reshape(1, D), w_ada=np.asarray(w_ada, f32)[0],
                  b_ada=np.asarray(b_ada, f32).reshape(1, 3 * D), w_in=win_p,
                  q_norm_w=np.asarray(q_norm_w, f32).reshape(1, 64), k_norm_w=np.asarray(k_norm_w, f32).reshape(1, 64),
                  w_pool=np.asarray(w_pool, f32)[0], pool_scale=np.asarray(pool_scale, f32).reshape(1, 512),
                  w_out=np.asarray(w_out, f32)[0], **consts)
    if 'S' in PHASES:
        shared.update(cache_k=ck, cache_v=cv, cache_ki=cki)
    in_maps = []
    for c in range(8):
        cvec = np.zeros((33, D), f32)
        cvec[0:16] = np.asarray(c_sample, f32)[c * 16:(c + 1) * 16]
        cvec[32] = np.asarray(c_prompt, f32)[c]
        m = dict(shared)
        m.update(x=np.ascontiguousarray(np.asarray(x_prompt, f32)[c]),
                 xs=np.ascontiguousarray(np.asarray(x_sample, f32)[c * 16:(c + 1) * 16, 0, :]),
                 cvec=cvec,
                 state_pool=np.ascontiguousarray(np.asarray(state_pool, f32)[0, c * 16:(c + 1) * 16].reshape(16, 15 * 512)),
                 page_table=np.ascontiguousarray(np.asarray(page_table, np.int32)[c * 16:(c + 1) * 16].reshape(1, 256)))
        in_maps.append(m)
    res = run_bass_kernel_spmd(nc, in_maps, core_ids=list(range(8)))
    r = res.results
    _CACHE['last'] = r
    cat = lambda k: np.stack([np.asarray(r[c][k], f32) for c in range(8)], 0)
    y_p = cat('y_p')
    y_s = cat('y_s').reshape(128, 1, D)
    k_p = cat('k_p').reshape(1, 8, T, 2, 64)
    v_p = cat('v_p').reshape(1, 8, T, 2, 64)
    ki_p = cat('ki_p').reshape(1, 8, T, 64)
    pool_p = cat('pool_p').reshape(1, 8, 15, 512)
    k_s = cat('k_s').reshape(1, 128, 1, 2, 64)
    v_s = cat('v_s').reshape(1, 128, 1, 2, 64)
    ki_s = cat('ki_s').reshape(1, 128, 1, 64)
    pool_s = cat('pool_s').reshape(1, 128, 15, 512)
    return (y_p, y_s, k_p, v_p, ki_p, pool_p, k_s, v_s, ki_s, pool_s)
```
